# Optimizing a Trainium2 kernel written in Bass

```python
import jax, jax.numpy as jnp
from jax import lax
import numpy as np

D_MODEL = 2048
BATCH = 2
SEQ = 16384
DEPTH = 2

GRID_W = 64
CTX_LEN = 256
MIX_WIDTH = D_MODEL
HGRN_WIDTH = MIX_WIDTH // 2
POOL_WIDTH = MIX_WIDTH - HGRN_WIDTH
HGRN_HEAD_DIM = 128
HGRN_HEADS = HGRN_WIDTH // HGRN_HEAD_DIM
HGRN_CHUNK = 64
POOL_WINDOWS = (2, 4, 8, 16)
POOL_GROUPS = len(POOL_WINDOWS)
POOL_GROUP_DIM = POOL_WIDTH // POOL_GROUPS
D_FF = 256 * ((8 * D_MODEL // 3 + 255) // 256)
N_MOD = 9
IN_WIDTH = 5 * HGRN_WIDTH + POOL_WIDTH
EPS = 1e-6
LOG_ZERO = -1e30

kernel_name = "hybrid_hgrn2_pool_macaron_dit"


def rms_norm(x, gain):
    xf = x.astype(jnp.float32)
    y = xf * lax.rsqrt(jnp.mean(xf * xf, axis=-1, keepdims=True) + EPS)
    return (y * gain.astype(jnp.float32)).astype(x.dtype)


def modulate(h, shift, scale):
    return h * (1 + scale) + shift


def swiglu(h, w_in, w_out):
    gate, up = jnp.split(h @ w_in, 2, axis=-1)
    return (jax.nn.silu(gate) * up) @ w_out


def ffn_sublayer(x, mod, j, gains, w_in, w_out):
    h = modulate(rms_norm(x, gains[2 * j]), mod[:, 3 * j, None, :], mod[:, 3 * j + 1, None, :])
    y = rms_norm(swiglu(h, w_in, w_out), gains[2 * j + 1])
    return x + 0.5 * mod[:, 3 * j + 2, None, :] * y


def gla_chunk_scan(q, k, v, log_f, s0):
    bsz, L, H, dk = q.shape
    dv = v.shape[-1]
    n = L // HGRN_CHUNK
    to_chunks = lambda a: jnp.moveaxis(a.reshape(bsz, n, HGRN_CHUNK, H, a.shape[-1]), 1, 0)
    mask = jnp.tril(jnp.ones((HGRN_CHUNK, HGRN_CHUNK), dtype=bool))

    def step(S, inp):
        qc, kc, vc, gc = inp
        b = jnp.cumsum(gc, axis=1)
        o_inter = jnp.einsum('bthk,bhkv->bthv', qc * jnp.exp(b), S)
        diff = b[:, :, None] - b[:, None, :]
        decay = jnp.where(mask[None, :, :, None, None], jnp.exp(jnp.minimum(diff, 0.0)), 0.0)
        attn = jnp.sum(qc[:, :, None] * kc[:, None] * decay, axis=-1)
        o_intra = jnp.einsum('btsh,bshv->bthv', attn, vc)
        b_end = b[:, -1]
        S_new = jnp.exp(b_end)[..., None] * S + jnp.einsum(
            'bshk,bshv->bhkv', kc * jnp.exp(b_end[:, None] - b), vc)
        return S_new, o_inter + o_intra

    s_final, o = lax.scan(step, s0, (to_chunks(q), to_chunks(k), to_chunks(v), to_chunks(log_f)))
    o = jnp.moveaxis(o, 0, 1).reshape(bsz, L, H, dv)
    return o, s_final


def hgrn2_direction(qh, vh, f_logit, lb, s0, reverse):
    bsz, L = f_logit.shape[:2]
    shp = (bsz, L, HGRN_HEADS, HGRN_HEAD_DIM)
    pos = lb > 0
    log_lb = jnp.where(pos, jnp.log(jnp.where(pos, lb, 1.0)), LOG_ZERO)
    log_f = jnp.logaddexp(jax.nn.log_sigmoid(f_logit),
                          log_lb + jax.nn.log_sigmoid(-f_logit)).reshape(shp)
    kh = ((1 - lb) * jax.nn.sigmoid(-f_logit)).reshape(shp)
    if reverse:
        qh, kh, vh, log_f = (jnp.flip(a, axis=1) for a in (qh, kh, vh, log_f))
    o, s = gla_chunk_scan(qh, kh, vh, log_f, s0)
    if reverse:
        o = jnp.flip(o, axis=1)
    return o, s


def hgrn2_mixer(q, f_fw, f_bw, iv, g, lb, gain, s0_fw, s0_bw):
    bsz, L, _ = q.shape
    shp = (bsz, L, HGRN_HEADS, HGRN_HEAD_DIM)
    qh = jax.nn.silu(q.astype(jnp.float32)).reshape(shp)
    vh = iv.astype(jnp.float32).reshape(shp)
    o_fw, s_fw = hgrn2_direction(qh, vh, f_fw.astype(jnp.float32), lb[0], s0_fw, False)
    o_bw, s_bw = hgrn2_direction(qh, vh, f_bw.astype(jnp.float32), lb[1], s0_bw, True)
    o = rms_norm(o_fw + o_bw, gain).reshape(bsz, L, HGRN_WIDTH)
    o = o * jax.nn.silu(g.astype(jnp.float32))
    return o.astype(q.dtype), s_fw, s_bw


def box_mean(v, w, axis):
    n = v.shape[axis]
    pad = [(0, 0)] * v.ndim
    pad[axis] = (1, 0)
    cs = jnp.pad(jnp.cumsum(v, axis=axis), pad)
    t = jnp.arange(n)
    lo = jnp.clip(t - w // 2, 0, n)
    hi = jnp.clip(t + w - w // 2, 0, n)
    s = jnp.take(cs, hi, axis=axis) - jnp.take(cs, lo, axis=axis)
    cnt_shape = [1] * v.ndim
    cnt_shape[axis] = n
    return s / (hi - lo).astype(v.dtype).reshape(cnt_shape)


def pool_mixer(p, rows, w_pool, b_pool, pool_scale):
    bsz, L, _ = p.shape
    groups = p.astype(jnp.float32).reshape(bsz, L, POOL_GROUPS, POOL_GROUP_DIM)
    outs = []
    for gi, w in enumerate(POOL_WINDOWS):
        v = groups[:, :, gi]
        if rows > 0:
            vg = v.reshape(bsz, rows, GRID_W, POOL_GROUP_DIM)
            m = box_mean(box_mean(vg, w, 1), w, 2).reshape(bsz, L, POOL_GROUP_DIM)
        else:
            m = box_mean(v, w, 1)
        outs.append(m - v)
    d = jnp.stack(outs, axis=2).astype(p.dtype)
    y = jnp.einsum('blgc,gcd->blgd', d, w_pool).reshape(bsz, L, POOL_WIDTH) + b_pool
    return y * pool_scale


def setup_inputs(seed: int = 0) -> dict:
    key = jax.random.key(seed)
    ks = jax.random.split(key, 16)
    f32 = jnp.float32
    nrm = lambda k, shp, s: jax.random.normal(k, shp, f32) * s
    return {
        "x": nrm(ks[0], (BATCH, SEQ, D_MODEL), 1.0),
        "c": nrm(ks[1], (BATCH, D_MODEL), 1.0),
        "ctx": nrm(ks[2], (BATCH, CTX_LEN, D_MODEL), 1.0),
        "c_ctx": nrm(ks[3], (D_MODEL,), 1.0),
        "w_ada": nrm(ks[4], (DEPTH, D_MODEL, N_MOD * D_MODEL), 0.5 * D_MODEL ** -0.5),
        "b_ada": nrm(ks[5], (DEPTH, N_MOD * D_MODEL), 0.02),
        "norm_gain": 1.0 + nrm(ks[6], (DEPTH, 6, D_MODEL), 0.05),
        "ffn_in": nrm(ks[7], (DEPTH, 2, D_MODEL, 2 * D_FF), D_MODEL ** -0.5),
        "ffn_out": nrm(ks[8], (DEPTH, 2, D_FF, D_MODEL), D_FF ** -0.5),
        "w_in": nrm(ks[9], (DEPTH, D_MODEL, IN_WIDTH), D_MODEL ** -0.5),
        "hgrn_lb": nrm(ks[10], (DEPTH, 2, HGRN_WIDTH), 0.5),
        "hgrn_gain": 1.0 + nrm(ks[11], (DEPTH, HGRN_HEAD_DIM), 0.05),
        "w_pool": nrm(ks[12], (DEPTH, POOL_GROUPS, POOL_GROUP_DIM, POOL_GROUP_DIM), POOL_GROUP_DIM ** -0.5),
        "b_pool": nrm(ks[13], (DEPTH, POOL_WIDTH), 0.02),
        "pool_scale": 1.0 + nrm(ks[14], (DEPTH, POOL_WIDTH), 0.1),
        "w_out": nrm(ks[15], (DEPTH, MIX_WIDTH, D_MODEL), MIX_WIDTH ** -0.5),
    }


def reference(x, c, ctx, c_ctx, w_ada, b_ada, norm_gain, ffn_in, ffn_out, w_in, hgrn_lb,
              hgrn_gain, w_pool, b_pool, pool_scale, w_out):
    bsz, seq = x.shape[0], x.shape[1]
    rows = seq // GRID_W
    split_at = [HGRN_WIDTH * k for k in range(1, 6)]
    lbp = jax.nn.softmax(hgrn_lb.astype(jnp.float32), axis=0)
    lower_bounds = jnp.cumsum(lbp, axis=0) - lbp[0]
    s0 = jnp.zeros((bsz, HGRN_HEADS, HGRN_HEAD_DIM, HGRN_HEAD_DIM), jnp.float32)
    for l in range(DEPTH):
        last = l == DEPTH - 1
        gains = norm_gain[l]
        mod_x = (jax.nn.silu(c) @ w_ada[l] + b_ada[l]).reshape(bsz, N_MOD, D_MODEL)
        mod_c = (jax.nn.silu(c_ctx)[None] @ w_ada[l] + b_ada[l]).reshape(1, N_MOD, D_MODEL)

        x = ffn_sublayer(x, mod_x, 0, gains, ffn_in[l, 0], ffn_out[l, 0])
        ctx = ffn_sublayer(ctx, mod_c, 0, gains, ffn_in[l, 0], ffn_out[l, 0])

        h_x = modulate(rms_norm(x, gains[2]), mod_x[:, 3, None, :], mod_x[:, 4, None, :])
        h_c = modulate(rms_norm(ctx, gains[2]), mod_c[:, 3, None, :], mod_c[:, 4, None, :])
        qx, ffx, fbx, ix, gx, px = jnp.split(h_x @ w_in[l], split_at, axis=-1)
        qc, ffc, fbc, ic, gc, pc = jnp.split(h_c @ w_in[l], split_at, axis=-1)
        o_c, s_fw, s_bw = hgrn2_mixer(qc, ffc, fbc, ic, gc, lower_bounds[l], hgrn_gain[l], s0, s0)
        o_x, _, _ = hgrn2_mixer(qx, ffx, fbx, ix, gx, lower_bounds[l], hgrn_gain[l], s_fw, s_bw)
        mix_x = jnp.concatenate(
            [o_x, pool_mixer(px, rows, w_pool[l], b_pool[l], pool_scale[l])], axis=-1) @ w_out[l]
        x = x + mod_x[:, 5, None, :] * rms_norm(mix_x, gains[3])
        if not last:
            mix_c = jnp.concatenate(
                [o_c, pool_mixer(pc, 0, w_pool[l], b_pool[l], pool_scale[l])], axis=-1) @ w_out[l]
            ctx = ctx + mod_c[:, 5, None, :] * rms_norm(mix_c, gains[3])
            ctx = ffn_sublayer(ctx, mod_c, 2, gains, ffn_in[l, 1], ffn_out[l, 1])

        x = ffn_sublayer(x, mod_x, 2, gains, ffn_in[l, 1], ffn_out[l, 1])
    return x
```

```python
import contextlib
import numpy as np
import concourse.bass as bass
import concourse.mybir as mybir
from concourse.bass_utils import run_bass_kernel_spmd

F32 = mybir.dt.float32
BF16 = mybir.dt.bfloat16
AF = mybir.ActivationFunctionType
ALU = mybir.AluOpType

D = 2048
KC = 16
DFF = 5632
JC = 44
TL = 16384
TCX = 256
T = TL + TCX
EPS = 1e-6
NCORE = 2
EPOCH = 8000
DLIM = 500
NBLK = TL // 128

C_SEL = 0
C_MFW = 2
C_MFWN = 10
C_MBW = 18
C_MBWN = 26
C_HTOP = 34
C_HBOT = 42
C_ID = 50
C_TRIF = 178
C_TRIB = 242
C_SMASK = 306
NCONST = 306 + 1024

POOL_W = (2, 4, 8, 16)
NTOP = (1, 1, 2, 4)
NBOT = (0, 1, 2, 4)


class Prog:
    ENGS = ("pe", "act", "dve", "pool", "sp")

    def __init__(self, nc, same_engine_sync=True):
        self.nc = nc
        self.ops = []
        self.last_w = {}
        self.readers = {}
        self.ses = same_engine_sync

    def op(self, eng, fn, r=(), w=(), kind="c", slot=None):
        idx = len(self.ops)
        deps = set()
        for k in r:
            if k in self.last_w:
                deps.add(self.last_w[k])
        for k in w:
            if k in self.last_w:
                deps.add(self.last_w[k])
            for rd in self.readers.get(k, ()):
                deps.add(rd)
        for k in r:
            self.readers.setdefault(k, []).append(idx)
        for k in w:
            self.last_w[k] = idx
            self.readers[k] = []
        if kind != "c" and slot is None:
            slot = (w[0] if (len(w) and kind == "dma_in") else (r[0] if len(r) else w[0]))
        self.ops.append(dict(eng=eng, fn=fn, deps=deps, kind=kind, slot=slot))
        return idx

    def barrier(self):
        self.ops.append(dict(eng=None, kind="barrier", deps=set(), fn=None, slot=None))
        self.last_w = {}
        self.readers = {}

    def emit(self):
        nc = self.nc
        eng_cnt = {e: 0 for e in self.ENGS}
        slot_n = {}
        for o in self.ops:
            if o["kind"] == "c":
                eng_cnt[o["eng"]] += 1
                o["ticket"] = eng_cnt[o["eng"]]
            elif o["kind"] in ("dma_in", "dma_out"):
                s = o["slot"]
                k = slot_n.get(s, 0)
                slot_n[s] = k + 1
                o["sep"] = k // DLIM
                o["dval"] = 16 * (k % DLIM + 1)
            elif o["kind"] == "barrier":
                o["snap_eng"] = dict(eng_cnt)
                o["snap_slot"] = dict(slot_n)
        with contextlib.ExitStack() as st:
            psem = {}
            for e in self.ENGS:
                nep = eng_cnt[e] // EPOCH + 1
                psem[e] = [st.enter_context(nc.semaphore(f"p_{e}_{i}")) for i in range(nep)]
            ssem = {}
            for i, (s, n) in enumerate(slot_n.items()):
                for ep in range((n - 1) // DLIM + 1):
                    ssem[(s, ep)] = st.enter_context(nc.semaphore(f"d_{i}_{ep}"))
            self.nsem = sum(len(v) for v in psem.values()) + len(ssem)
            block = st.enter_context(nc.Block())
            ops = self.ops
            ses = self.ses

            def run_engine(e, h):
                waited_eng = {x: 0 for x in self.ENGS}
                waited_slot = {}

                def wait_ticket(src, tk):
                    if tk <= waited_eng[src]:
                        return
                    ep = (tk - 1) // EPOCH
                    h.wait_ge(psem[src][ep], tk - ep * EPOCH)
                    waited_eng[src] = tk

                def wait_slot(s, ep, v):
                    if (ep, v) <= waited_slot.get(s, (-1, 0)):
                        return
                    h.wait_ge(ssem[(s, ep)], v)
                    waited_slot[s] = (ep, v)

                for o in ops:
                    if o["kind"] == "barrier":
                        for src, tk in o["snap_eng"].items():
                            if src != e and tk > 0:
                                wait_ticket(src, tk)
                        for s, n in o["snap_slot"].items():
                            wait_slot(s, (n - 1) // DLIM, 16 * ((n - 1) % DLIM + 1))
                        continue
                    if o["eng"] != e:
                        continue
                    for di in sorted(o["deps"]):
                        dop = ops[di]
                        if dop["kind"] == "c":
                            if dop["eng"] == e:
                                if e == "pe" or e == "sp" or not ses:
                                    continue
                            wait_ticket(dop["eng"], dop["ticket"])
                        else:
                            wait_slot(dop["slot"], dop["sep"], dop["dval"])
                    ins = o["fn"](h)
                    if o["kind"] == "c":
                        tk = o["ticket"]
                        ep = (tk - 1) // EPOCH
                        ins.then_inc(psem[e][ep], 1)
                    else:
                        ins.then_inc(ssem[(o["slot"], o["sep"])], 16)
                if e == "sp":
                    for s, n in slot_n.items():
                        wait_slot(s, (n - 1) // DLIM, 16 * ((n - 1) % DLIM + 1))

            @block.sync
            def _(h):
                run_engine("sp", h)

            @block.tensor
            def _(h):
                run_engine("pe", h)

            @block.scalar
            def _(h):
                run_engine("act", h)

            @block.vector
            def _(h):
                run_engine("dve", h)

            @block.gpsimd
            def _(h):
                run_engine("pool", h)


def _pool_tables():
    mats = []
    idx = {}
    tt = np.arange(128)
    rl_t, c_t = tt // 64, tt % 64

    def add(key, m):
        idx[key] = len(mats)
        mats.append(m.astype(np.float32))

    for g, w in enumerate(POOL_W):
        half = w // 2
        nd = {1: (-1, 0), 2: (-1, 0, 1), 4: (-2, -1, 0, 1, 2), 8: tuple(range(-4, 5))}[half]
        for dl in nd:
            rs = (2 * dl + rl_t)[:, None]
            rt = rl_t[None, :]
            cs = c_t[:, None]
            ct = c_t[None, :]
            m = ((rs >= rt - half) & (rs < rt + half) & (cs >= ct - half) & (cs < ct + half)).astype(np.float32)
            if dl != 0:
                add(("g", g, dl), m)
            else:
                classes = [("mid", 8)] + [("top", k) for k in range(NTOP[g])] + [("bot", k) for k in range(NBOT[g])]
                for cls, k in classes:
                    ob = 8 if cls == "mid" else (k if cls == "top" else NBLK - NBOT[g] + k)
                    rg = 2 * ob + rl_t
                    cnt_r = np.minimum(256, rg + half) - np.maximum(0, rg - half)
                    cnt_c = np.minimum(64, c_t + half) - np.maximum(0, c_t - half)
                    mm = m.copy()
                    mm[tt, tt] -= (cnt_r * cnt_c)
                    add(("d", g, cls, k), mm)
    for g, w in enumerate(POOL_W):
        half = w // 2
        for ob in range(2):
            for ib in range(2):
                sidx = (128 * ib + tt)[:, None]
                t = (128 * ob + tt)[None, :]
                m = ((sidx >= t - half) & (sidx < t + half)).astype(np.float32)
                if ib == ob:
                    tg = 128 * ob + tt
                    cnt = np.minimum(256, tg + half) - np.maximum(0, tg - half)
                    m[tt, tt] -= cnt
                add(("c", g, ob, ib), m)
    pm = np.stack(mats, 0)
    inv = np.zeros((4, T), np.float32)
    tl = np.arange(TL)
    rg = tl // 64
    cc = tl % 64
    tcx = np.arange(TCX)
    for g, w in enumerate(POOL_W):
        half = w // 2
        cnt_r = np.minimum(256, rg + half) - np.maximum(0, rg - half)
        cnt_c = np.minimum(64, cc + half) - np.maximum(0, cc - half)
        inv[g, :TL] = 1.0 / (cnt_r * cnt_c)
        cnt = np.minimum(256, tcx + half) - np.maximum(0, tcx - half)
        inv[g, TL:] = 1.0 / cnt
    return pm, idx, inv


def _pool_mat_index():
    _, idx, _ = _pool_tables()
    return idx


def _mat_for(idx, g, ob, dl):
    if dl != 0:
        return idx[("g", g, dl)]
    if ob < NTOP[g]:
        return idx[("d", g, "top", ob)]
    if ob >= NBLK - NBOT[g]:
        return idx[("d", g, "bot", ob - (NBLK - NBOT[g]))]
    return idx[("d", g, "mid", 8)]


def _tile_weights(ffn_in, ffn_out, w_in, w_out, w_pool):
    out = {}
    L = ffn_in.shape[0]
    fi = np.empty((L, 2, JC, 128, KC, 256), np.float32)
    fo = np.empty((L, 2, KC, 128, JC, 128), np.float32)
    for l in range(L):
        for f in range(2):
            W = ffn_in[l, f].reshape(KC, 128, 2, JC, 128)
            fi[l, f] = W.transpose(3, 1, 0, 2, 4).reshape(JC, 128, KC, 256)
            W2 = ffn_out[l, f].reshape(JC, 128, KC, 128)
            fo[l, f] = W2.transpose(2, 1, 0, 3)
    out["ffn_in"] = fi.reshape(L, 2, -1)
    out["ffn_out"] = fo.reshape(L, 2, -1)
    wi = np.empty((L, D * 6144), np.float32)
    wo = np.empty((L, KC, 128, KC, 128), np.float32)
    fm_src = list(range(0, 24)) + list(range(32, 40))
    for l in range(L):
        W = w_in[l].reshape(KC, 128, 48, 128)
        fm = W[:, :, fm_src, :].transpose(2, 1, 0, 3)
        tmb = []
        for c0 in (24, 28, 40, 44):
            blk = W[:, :, c0:c0 + 4, :].reshape(KC, 128, 512).transpose(1, 0, 2)
            tmb.append(blk.reshape(-1))
        wi[l] = np.concatenate([fm.reshape(-1)] + tmb)
        W3 = w_out[l].reshape(KC, 128, KC, 128)
        wo[l] = W3.transpose(2, 1, 0, 3)
    out["w_in"] = wi
    out["w_out"] = wo.reshape(L, -1)
    wp = w_pool.reshape(L, 4, 2, 128, 256).transpose(0, 3, 1, 2, 4)
    out["w_pool"] = np.ascontiguousarray(wp).reshape(L, -1)
    return out


WSPEC = [
    ("ffn_in", D * 2 * DFF),
    ("ffn_out", DFF * D),
    ("w_in", D * 6144),
    ("w_out", D * D),
    ("w_pool", 4 * 256 * 256),
]


def build_program(dbg=(), stop_after=None, ses=False, ext_in=(), mode=None):
    nc = bass.Bass("TRN2", target_bir_lowering=False)
    P = Prog(nc, same_engine_sync=ses)
    midx = _pool_mat_index()
    NMAT = len(midx)

    def din(name, shape, dt=F32):
        return nc.dram_tensor(name, list(shape), dt, kind="ExternalInput").ap()

    def dscr(name, shape, dt=F32):
        if name in ext_in:
            return nc.dram_tensor(name, list(shape), dt, kind="ExternalInput").ap()
        if name in dbg:
            return nc.dram_tensor(name, list(shape), dt, kind="ExternalOutput").ap()
        return nc.dram_tensor(name, list(shape), dt).ap()

    _inp = {}
    _shapes = {
        "xT": [D, T], "ccT": [128, KC, 2], "wada": [2, D, 9 * D], "bada": [128, 2, 144], "gainsT": [128, 2, 6, KC],
        "lbT": [128, 2, 2, 8], "hgainT": [128, 2], "bpoolT": [128, 2, 8], "pscaleT": [128, 2, 8],
        "consts": [128, NCONST], "pmat": [128, NMAT, 128], "invcnt": [128, 4, T],
    }
    for name, E in WSPEC:
        nslot = 2 if name.startswith("ffn") else 1
        _shapes[name + "_t"] = [2, nslot, 128, E // 128]

    def INP(name):
        if name not in _inp:
            _inp[name] = din(name, _shapes[name])
        return _inp[name]

    outT = nc.dram_tensor("outT", [D, TL], F32, kind="ExternalOutput").ap()

    xs = dscr("xs", [D, T])
    qT = dscr("qT", [1024, T])
    sgT = dscr("sgT", [1024, T])
    lfT = [dscr(f"lfT{d}", [1024, T]) for d in range(2)]
    kT = [dscr(f"kT{d}", [1024, T]) for d in range(2)]
    vtok = dscr("vtok", [T, 1024], BF16)
    ptok = dscr("ptok", [T, 1024], BF16)
    ofw = dscr("ofw", [1024, T])
    catT = dscr("catT", [D, T], BF16)
    dbg_pool_t = [dscr("dbg_pool", [1024, T], BF16)] if "dbg_pool" in dbg else []
    wgat = {}
    for name, E in WSPEC:
        nslot = 2 if name.startswith("ffn") else 1
        for l in range(2):
            for f in range(nslot):
                wgat[(name, l, f)] = nc.dram_tensor(f"wb_{name}_{l}_{f}", [128, E // 128], BF16).ap()

    def wtiles(name, l, f, per_part):
        return wgat[(name, l, f)].rearrange("r x -> (r x)").rearrange("(t p x) -> t p x", p=128, x=per_part)

    es = contextlib.ExitStack()
    _uid = [0]

    def uname(name):
        _uid[0] += 1
        return f"s{_uid[0]}_{name}"

    def done():
        P.emit()
        return nc

    with es:
        def sb(name, shape, dt=F32):
            return es.enter_context(nc.sbuf_tensor(uname(name), list(shape), dt))

        def pst(name, shape, dt=F32):
            return es.enter_context(nc.psum_tensor(uname(name), list(shape), dt))

        cst = sb("cst", [128, NCONST])
        ident = sb("ident", [128, 128], BF16)
        ones = sb("ones", [128, 128], BF16)
        trim = sb("trim", [64, 2, 64])
        gains = sb("gains", [128, 2, 6, KC])
        oml = sb("oml", [128, 2, 2, 8])
        hgain = sb("hgain", [128, 2])
        bpool = sb("bpool", [128, 2, 8])
        pscale = sb("pscale", [128, 2, 8])
        Mx = sb("Mx", [128, 2, 144])
        Mc = sb("Mc", [128, 2, 144])
        tabs = sb("tabs", [128, 2, 3, 2, 2, KC])
        eps_t = sb("eps_t", [128, 1])
        psA = [pst(f"psA{i}", [128, 512]) for i in range(7)]
        psB = pst("psB", [128, 1024], BF16)

        deferred = []
        for i_, (dst_, nm_) in enumerate(((cst, "consts"), (gains, "gainsT"), (hgain, "hgainT"), (bpool, "bpoolT"), (pscale, "pscaleT"))):
            P.op("sp", lambda h, dst_=dst_, nm_=nm_: h.dma_start(out=dst_[:], in_=INP(nm_)), w=[nm_], kind="dma_in", slot=("ldc", i_))
        P.op("pool", lambda h: h.memset(ones[:], 1.0), w=["ones"])
        P.op("pool", lambda h: h.memset(eps_t[:], EPS), w=["eps"])
        P.op("dve", lambda h: h.tensor_copy(out=ident[:], in_=cst[:, C_ID:C_ID + 128]), r=["consts"], w=["ident"])
        P.op("dve", lambda h: h.tensor_copy(out=trim[:, 0, :], in_=cst[0:64, C_TRIF:C_TRIF + 64]), r=["consts"], w=["trim0"])
        P.op("dve", lambda h: h.tensor_copy(out=trim[:, 1, :], in_=cst[0:64, C_TRIB:C_TRIB + 64]), r=["consts"], w=["trim1"])

        if mode != "bonly":
            with contextlib.ExitStack() as ps0:
                def sb0(name, shape, dt=F32):
                    return ps0.enter_context(nc.sbuf_tensor(uname(name), list(shape), dt))
                cc = sb0("cc", [128, KC, 2])
                scc = sb0("scc", [128, KC, 2])
                wad = sb0("wad", [128, 2, KC, 512])
                bada = sb0("bada", [128, 2, 144])
                lbt = sb0("lbt", [128, 2, 2, 8])
                lbd = sb0("lbd", [128, 2, 8])
                cstg = sb0("cstg", [128, 2, 4096])
                cbf = sb0("cbf", [128, 2, 4096], BF16)

                P.op("sp", lambda h: h.dma_start(out=lbt[:], in_=INP("lbT")), w=["lbt"], kind="dma_in", slot="ld_lb")
                P.op("dve", lambda h: h.tensor_tensor(out=lbd[:], in0=lbt[:, 1], in1=lbt[:, 0], op=ALU.subtract), r=["lbt"], w=["lbd"])
                P.op("act", lambda h: h.activation(out=lbd[:], in_=lbd[:], func=AF.Sigmoid), r=["lbd"], w=["lbd"])
                P.op("pool", lambda h: h.memset(oml[:, 0], 1.0), w=["oml0"])
                P.op("dve", lambda h: h.tensor_scalar(out=oml[:, 1], in0=lbd[:], scalar1=-1.0, scalar2=1.0, op0=ALU.mult, op1=ALU.add),
                     r=["lbd"], w=["oml1"])

                P.op("sp", lambda h: h.dma_start(out=cc[:], in_=INP("ccT")), w=["cc"], kind="dma_in", slot="ld_cc")
                P.op("sp", lambda h: h.dma_start(out=bada[:], in_=INP("bada")), w=["bada"], kind="dma_in", slot="ld_bada")
                P.op("act", lambda h: h.activation(out=scc[:], in_=cc[:], func=AF.Silu), r=["cc"], w=["scc"])
                for l in range(2):
                    for pc in range(36):
                        bufi = (l * 36 + pc) % 2
                        src = INP("wada")[l].rearrange("(kc p) n -> p kc n", p=128)[:, :, pc * 512:(pc + 1) * 512]
                        P.op("sp", lambda h, bufi=bufi, src=src: h.dma_start(out=wad[:, bufi], in_=src),
                             w=[("wad", bufi)], kind="dma_in", slot=("wad", bufi))

                        def mm(h, l=l, pc=pc, bufi=bufi):
                            ins = None
                            for c4 in range(4):
                                c = pc * 4 + c4
                                for kc in range(KC):
                                    ins = h.matmul(psA[5 + l][:, c * 2:c * 2 + 2],
                                                   lhsT=wad[:, bufi, kc, c4 * 128:(c4 + 1) * 128], rhs=scc[:, kc, :],
                                                   start=(kc == 0), stop=(kc == KC - 1))
                            return ins
                        P.op("pe", mm, r=[("wad", bufi), "scc"], w=[("ps", 5 + l)])
                for l in range(2):
                    pv = psA[5 + l][:, 0:288].rearrange("p (c b) -> p c b", b=2)
                    P.op("dve", lambda h, l=l, pv=pv: h.tensor_tensor(out=Mx[:, l, :], in0=pv[:, :, 0], in1=bada[:, l, :], op=ALU.add),
                         r=[("ps", 5 + l), "bada"], w=[("Mx", l)])
                    P.op("dve", lambda h, l=l, pv=pv: h.tensor_tensor(out=Mc[:, l, :], in0=pv[:, :, 1], in1=bada[:, l, :], op=ALU.add),
                         r=[("ps", 5 + l), "bada"], w=[("Mc", l)])
                allM = [("Mx", l) for l in range(2)] + [("Mc", l) for l in range(2)]
                for l in range(2):
                    for sub in range(3):
                        n0 = 3 * sub
                        gi, go = 2 * sub, 2 * sub + 1
                        cf = 1.0 if sub == 1 else 0.5
                        for wh, M in enumerate((Mx, Mc)):
                            scale = M[:, l, (n0 + 1) * 16:(n0 + 2) * 16]
                            gate = M[:, l, (n0 + 2) * 16:(n0 + 3) * 16]
                            P.op("dve", lambda h, l=l, sub=sub, wh=wh, scale=scale, gi=gi: h.scalar_tensor_tensor(
                                out=tabs[:, l, sub, wh, 0, :], in0=scale, scalar=1.0, in1=gains[:, l, gi, :], op0=ALU.add, op1=ALU.mult),
                                r=allM + ["gainsT"], w=[("tabs", l, sub, wh, 0)])
                            P.op("dve", lambda h, l=l, sub=sub, wh=wh, gate=gate, go=go, cf=cf: h.scalar_tensor_tensor(
                                out=tabs[:, l, sub, wh, 1, :], in0=gate, scalar=cf, in1=gains[:, l, go, :], op0=ALU.mult, op1=ALU.mult),
                                r=allM + ["gainsT"], w=[("tabs", l, sub, wh, 1)])
                if stop_after == "mod":
                    dbg_tabs = nc.dram_tensor("dbg_tabs", [128, 2 * 3 * 2 * 2 * KC], F32, kind="ExternalOutput").ap()
                    P.op("sp", lambda h: h.dma_start(out=dbg_tabs, in_=tabs[:].rearrange("p a b c d e -> p (a b c d e)")),
                         r=[("tabs", l, sub, wh, i) for l in range(2) for sub in range(3) for wh in range(2) for i in range(2)],
                         w=["dbgt"], kind="dma_out", slot="dbgt")
                    return done()

                order = []
                for l in range(2):
                    order += [("ffn_in", l, 0), ("ffn_out", l, 0), ("w_in", l, 0), ("w_pool", l, 0), ("w_out", l, 0),
                              ("ffn_in", l, 1), ("ffn_out", l, 1)]
                import os as _os
                if _os.environ.get("KDEBUG_NW"):
                    order = order[:int(_os.environ["KDEBUG_NW"])]
                pcs = 0
                if not _os.environ.get("KDEBUG_NW"):
                    for (name, l, f) in order[3:]:
                        Fw = dict(WSPEC)[name] // 128
                        for c0 in range(0, Fw, 4096):
                            deferred.append((name, l, f, c0, min(4096, Fw - c0)))
                    order = order[:3]
                for (name, l, f) in order:
                    Fw = dict(WSPEC)[name] // 128
                    src_all = INP(name + "_t")[l, f]
                    for c0 in range(0, Fw, 4096):
                        cw = min(4096, Fw - c0)
                        bufi = pcs % 2
                        eng = ("dve", "act", "pool")[pcs % 3]
                        pcs += 1
                        P.op("sp", lambda h, bufi=bufi, c0=c0, cw=cw, src_all=src_all: h.dma_start(
                            out=cstg[:, bufi, :cw], in_=src_all[:, c0:c0 + cw]), w=[("cstg", bufi)], kind="dma_in", slot=("cstg", bufi))
                        if eng == "act":
                            P.op("act", lambda h, bufi=bufi, cw=cw: h.activation(out=cbf[:, bufi, :cw], in_=cstg[:, bufi, :cw], func=AF.Copy),
                                 r=[("cstg", bufi)], w=[("cbf", bufi)])
                        else:
                            P.op(eng, lambda h, bufi=bufi, cw=cw: h.tensor_copy(out=cbf[:, bufi, :cw], in_=cstg[:, bufi, :cw]),
                                 r=[("cstg", bufi)], w=[("cbf", bufi)])
                        P.op("sp", lambda h, bufi=bufi, c0=c0, cw=cw, name=name, l=l, f=f: h.dma_start(
                            out=wgat[(name, l, f)][:, c0:c0 + cw], in_=cbf[:, bufi, :cw]),
                            r=[("cbf", bufi)], w=[("wl", name, l, f, c0)], kind="dma_out", slot=("cbfst", bufi))
        P.barrier()
        if "tabs" in dbg:
            dbg_w = nc.dram_tensor("dbg_w", [128, KC * 256], BF16, kind="ExternalOutput").ap()
            P.op("sp", lambda h: h.dma_start(out=dbg_w, in_=wtiles("ffn_in", 0, 0, KC * 256)[3]), w=["dbgw"], kind="dma_out", slot="dbgw")
        if stop_after == "prologue":
            return done()

        TILES = [(1024 * i, 1024, 0) for i in range(TL // 1024)] + [(TL, TCX, 1)]

        def subs_of(n):
            return [(0, 512), (512, 512)] if n == 1024 else [(0, n)]

        class AC:
            pass

        def alloc_ac(stack, wbuf_cols):
            a = AC()

            def sbx(name, shape, dt=F32):
                return stack.enter_context(nc.sbuf_tensor(uname(name), list(shape), dt))
            a.hbuf = sbx("hbuf", [128, KC, 1024], BF16)
            a.hid = sbx("hid", [128, 11, 1024], BF16)
            a.ybuf = sbx("ybuf", [128, KC, 1024])
            a.xst = sbx("xst", [128, 2, 1024])
            a.sq = sbx("sq", [128, 2, 1024], BF16)
            a.rstd = sbx("rstd", [128, 1024])
            a.tmp = sbx("tmp", [128, 2, 1024])
            a.wflat = sbx("wbuf", [128, wbuf_cols], BF16)
            a.nslots = wbuf_cols // 2048
            a.wcnt = 0
            a.pscnt = 0
            a.tcnt = 0
            a.xcnt = 0
            a.ycnt = 0
            return a

        def rms_stats(a, n, srcs, tag):
            sbs = subs_of(n)
            for kc in range(KC):
                b = kc % 2
                P.op("act", lambda h, kc=kc, b=b: h.activation(out=a.sq[:, b, :n], in_=a.ybuf[:, kc, :n], func=AF.Square),
                     r=[("y", kc)], w=[("sq", b)])

                def mm(h, kc=kc, b=b):
                    ins = None
                    for si, (s0, sn) in enumerate(sbs):
                        ins = h.matmul(psA[5 + si][:, :sn], lhsT=ones[:], rhs=a.sq[:, b, s0:s0 + sn],
                                       start=(kc == 0), stop=(kc == KC - 1))
                    return ins
                P.op("pe", mm, r=[("sq", b), "ones"], w=[("ps", 5), ("ps", 6)])
            for si, (s0, sn) in enumerate(sbs):
                P.op("act", lambda h, si=si, s0=s0, sn=sn: h.activation(out=a.rstd[:, s0:s0 + sn], in_=psA[5 + si][:, :sn],
                                                                      func=AF.Sqrt, scale=1.0 / D, bias=eps_t[:, 0:1]),
                     r=[("ps", 5), ("ps", 6), "eps"], w=[("rstd", si)])
                P.op("dve", lambda h, s0=s0, sn=sn: h.reciprocal(out=a.rstd[:, s0:s0 + sn], in_=a.rstd[:, s0:s0 + sn]),
                     r=[("rstd", si)], w=[("rstd", si)])

        def modulate_to_h(a, n, l, sub, wh):
            M = Mx if wh == 0 else Mc
            for kc in range(KC):
                b = a.tcnt % 2
                a.tcnt += 1
                P.op("dve", lambda h, kc=kc, b=b: h.tensor_tensor(out=a.tmp[:, b, :n], in0=a.ybuf[:, kc, :n], in1=a.rstd[:, :n], op=ALU.mult),
                     r=[("y", kc), ("rstd", 0), ("rstd", 1)], w=[("tmp", b)])
                P.op("act", lambda h, kc=kc, b=b: h.activation(
                    out=a.hbuf[:, kc, :n], in_=a.tmp[:, b, :n], func=AF.Identity,
                    scale=tabs[:, l, sub, wh, 0, kc:kc + 1], bias=M[:, l, 3 * sub * 16 + kc:3 * sub * 16 + kc + 1]),
                    r=[("tmp", b)], w=[("h", kc)])

        WSLOT = 2048

        def load_w(a, src_ap, ncols, wkey):
            k = (ncols + WSLOT - 1) // WSLOT
            if a.wcnt + k > a.nslots:
                a.wcnt = 0
            s0 = a.wcnt
            a.wcnt += k
            keys = [("w", s0 + i) for i in range(k)]
            off = s0 * WSLOT
            P.op("sp", lambda h, off=off: h.dma_start(out=a.wflat[:, off:off + ncols], in_=src_ap), w=keys, kind="dma_in", slot=("w", s0))
            return (off, keys)

        def ffn(a, l, f, tile, src, dst, x_in_y, ti):
            c0, n, wh = tile
            sbs = subs_of(n)
            sub = 0 if f == 0 else 2
            if not x_in_y:
                for g4 in range(4):
                    P.op("sp", lambda h, g4=g4: h.dma_start(
                        out=a.ybuf[:, 4 * g4:4 * g4 + 4, :n],
                        in_=src[512 * g4:512 * g4 + 512, c0:c0 + n].rearrange("(k p) t -> p k t", p=128)),
                        w=[("y", 4 * g4 + i) for i in range(4)], kind="dma_in", slot=("yld", g4))
            rms_stats(a, n, None, "pre")
            modulate_to_h(a, n, l, sub, wh)
            wi_t = wtiles("ffn_in", l, f, KC * 256)
            wo_t = wtiles("ffn_out", l, f, JC * 128)
            for q4 in range(4):
                for jj in range(11):
                    j = q4 * 11 + jj
                    wb = load_w(a, wi_t[j], KC * 256, ("wg", "ffn_in", l, f))
                    for (s0, sn) in sbs:
                        pg = (a.pscnt % 2) * 2
                        a.pscnt += 1

                        def mm(h, wb=wb, s0=s0, sn=sn, pg=pg):
                            ins = None
                            for u in range(2):
                                for kc in range(KC):
                                    ins = h.matmul(psA[pg + u][:, :sn], lhsT=a.wflat[:, wb[0] + kc * 256 + u * 128:wb[0] + kc * 256 + u * 128 + 128],
                                                   rhs=a.hbuf[:, kc, s0:s0 + sn], start=(kc == 0), stop=(kc == KC - 1))
                            return ins
                        P.op("pe", mm, r=wb[1] + [("h", kc) for kc in range(KC)], w=[("ps", pg), ("ps", pg + 1)])
                        b = a.tcnt % 2
                        a.tcnt += 1
                        P.op("act", lambda h, pg=pg, sn=sn, b=b: h.activation(out=a.tmp[:, b, :sn], in_=psA[pg][:, :sn], func=AF.Silu),
                             r=[("ps", pg)], w=[("tmp", b)])
                        P.op("dve", lambda h, pg=pg, sn=sn, b=b, jj=jj, s0=s0: h.tensor_tensor(
                            out=a.hid[:, jj, s0:s0 + sn], in0=a.tmp[:, b, :sn], in1=psA[pg + 1][:, :sn], op=ALU.mult),
                            r=[("tmp", b), ("ps", pg + 1)], w=[("hid", jj, s0)])
                for m in range(KC):
                    wb = load_w(a, wo_t[m][:, q4 * 11 * 128:(q4 + 1) * 11 * 128], 11 * 128, ("wg", "ffn_out", l, f))
                    for (s0, sn) in sbs:
                        pg = 4 + 0 * (a.pscnt % 2)
                        pg = (a.pscnt % 2) * 2 + (a.pscnt // 2) % 2
                        a.pscnt += 1

                        def mm2(h, wb=wb, s0=s0, sn=sn, pg=pg):
                            ins = None
                            for jj in range(11):
                                ins = h.matmul(psA[pg][:, :sn], lhsT=a.wflat[:, wb[0] + jj * 128:wb[0] + (jj + 1) * 128],
                                               rhs=a.hid[:, jj, s0:s0 + sn], start=(jj == 0), stop=(jj == 10))
                            return ins
                        P.op("pe", mm2, r=wb[1] + [("hid", jj, s0) for jj in range(11)], w=[("ps", pg)])
                        if q4 == 0:
                            P.op("act", lambda h, pg=pg, m=m, s0=s0, sn=sn: h.activation(out=a.ybuf[:, m, s0:s0 + sn], in_=psA[pg][:, :sn], func=AF.Copy),
                                 r=[("ps", pg)], w=[("y", m)])
                        else:
                            P.op("dve", lambda h, pg=pg, m=m, s0=s0, sn=sn: h.tensor_tensor(
                                out=a.ybuf[:, m, s0:s0 + sn], in0=a.ybuf[:, m, s0:s0 + sn], in1=psA[pg][:, :sn], op=ALU.add),
                                r=[("ps", pg), ("y", m)], w=[("y", m)])
            residual(a, l, sub, tile, src, dst, ti)

        def residual(a, l, sub, tile, src, dst, ti, dst_cols=None):
            c0, n, wh = tile
            rms_stats(a, n, None, "post")
            for m in range(KC):
                xb = a.xcnt % 2
                a.xcnt += 1
                P.op("sp", lambda h, m=m, xb=xb: h.dma_start(out=a.xst[:, xb, :n], in_=src[m * 128:(m + 1) * 128, c0:c0 + n]),
                     r=[("dx", ti, m)], w=[("xst", xb)], kind="dma_in", slot=("xst", xb))
                b = a.tcnt % 2
                a.tcnt += 1
                P.op("dve", lambda h, m=m, b=b: h.tensor_tensor(out=a.tmp[:, b, :n], in0=a.ybuf[:, m, :n], in1=a.rstd[:, :n], op=ALU.mult),
                     r=[("y", m), ("rstd", 0), ("rstd", 1)], w=[("tmp", b)])
                P.op("dve", lambda h, m=m, b=b, xb=xb: h.scalar_tensor_tensor(
                    out=a.ybuf[:, m, :n], in0=a.tmp[:, b, :n], scalar=tabs[:, l, sub, wh, 1, m:m + 1], in1=a.xst[:, xb, :n],
                    op0=ALU.mult, op1=ALU.add), r=[("tmp", b), ("xst", xb)], w=[("y", m)])
                if dst is not None:
                    dc0 = c0 if dst_cols is None else dst_cols
                    P.op("sp", lambda h, m=m, dc0=dc0: h.dma_start(out=dst[m * 128:(m + 1) * 128, dc0:dc0 + n], in_=a.ybuf[:, m, :n]),
                         r=[("y", m)], w=[("dx", ti, m)], kind="dma_out", slot=("yst", m % 4))

        def phase_a(l):
            with contextlib.ExitStack() as stA:
                a = alloc_ac(stA, 24576)
                src = INP("xT") if l == 0 else xs
                w_fm = wgat[("w_in", l, 0)].rearrange("r x -> (r x)")[0:32 * 128 * 2048].rearrange("(t p x) -> t p x", p=128, x=2048)
                w_tm = wgat[("w_in", l, 0)].rearrange("r x -> (r x)")[32 * 128 * 2048:].rearrange("(t p x) -> t p x", p=128, x=8192)
                def a_tile(ti, tile):
                    c0, n, wh = tile
                    sbs = subs_of(n)
                    ffn(a, l, 0, tile, src, xs, False, ti)
                    rms_stats(a, n, None, "mix")
                    modulate_to_h(a, n, l, 1, wh)
                    for fc in range(32):
                        typ = fc // 8
                        hh = fc % 8
                        wb = load_w(a, w_fm[fc], 2048, ("wg", "w_in", l, 0))
                        ob = a.ycnt % 16
                        a.ycnt += 1
                        ob2 = a.ycnt % 16
                        for (s0, sn) in sbs:
                            pg = a.pscnt % 4
                            a.pscnt += 1

                            def mm(h, wb=wb, s0=s0, sn=sn, pg=pg):
                                ins = None
                                for kc in range(KC):
                                    ins = h.matmul(psA[pg][:, :sn], lhsT=a.wflat[:, wb[0] + kc * 128:wb[0] + (kc + 1) * 128],
                                                   rhs=a.hbuf[:, kc, s0:s0 + sn], start=(kc == 0), stop=(kc == KC - 1))
                                return ins
                            P.op("pe", mm, r=wb[1] + [("h", kc) for kc in range(KC)], w=[("ps", pg)])
                            if typ in (0, 3):
                                P.op("act", lambda h, pg=pg, ob=ob, s0=s0, sn=sn: h.activation(
                                    out=a.ybuf[:, ob, s0:s0 + sn], in_=psA[pg][:, :sn], func=AF.Silu), r=[("ps", pg)], w=[("y", ob)])
                            else:
                                d = typ - 1
                                b = a.tcnt % 2
                                a.tcnt += 1
                                P.op("act", lambda h, pg=pg, b=b, sn=sn: h.activation(
                                    out=a.tmp[:, b, :sn], in_=psA[pg][:, :sn], func=AF.Sigmoid, scale=-1.0), r=[("ps", pg)], w=[("tmp", b)])
                                P.op("dve", lambda h, b=b, ob=ob, s0=s0, sn=sn, d=d, hh=hh: h.tensor_scalar(
                                    out=a.ybuf[:, ob, s0:s0 + sn], in0=a.tmp[:, b, :sn], scalar1=oml[:, l, d, hh:hh + 1], scalar2=None, op0=ALU.mult),
                                    r=[("tmp", b)], w=[("y", ob)])
                        if typ in (0, 3):
                            dstT = qT if typ == 0 else sgT
                            P.op("sp", lambda h, ob=ob, hh=hh, dstT=dstT: h.dma_start(out=dstT[hh * 128:(hh + 1) * 128, c0:c0 + n], in_=a.ybuf[:, ob, :n]),
                                 r=[("y", ob)], w=[("dq", typ, hh, ti)], kind="dma_out", slot=("pst", ob % 4))
                        else:
                            d = typ - 1
                            a.ycnt += 1
                            P.op("sp", lambda h, ob=ob, hh=hh, d=d: h.dma_start(out=kT[d][hh * 128:(hh + 1) * 128, c0:c0 + n], in_=a.ybuf[:, ob, :n]),
                                 r=[("y", ob)], w=[("dk", d, hh, ti)], kind="dma_out", slot=("pst", ob % 4))
                            P.op("act", lambda h, ob=ob, ob2=ob2: h.activation(out=a.ybuf[:, ob2, :n], in_=a.ybuf[:, ob, :n], func=AF.Ln, scale=-1.0, bias=1.0),
                                 r=[("y", ob)], w=[("y", ob2)])
                            P.op("sp", lambda h, ob2=ob2, hh=hh, d=d: h.dma_start(out=lfT[d][hh * 128:(hh + 1) * 128, c0:c0 + n], in_=a.ybuf[:, ob2, :n]),
                                 r=[("y", ob2)], w=[("dlf", d, hh, ti)], kind="dma_out", slot=("pst", ob2 % 4))
                    ntb = n // 128
                    for blk in range(4):
                        wb = load_w(a, w_tm[blk], 8192, ("wg", "w_in", l, 0))
                        dstD = vtok if blk < 2 else ptok
                        colo = (blk % 2) * 512
                        vb = blk % 2
                        vst = a.hid[:, 4 * vb:4 * vb + 4, :].rearrange("p a (b c) -> p (a b) c", c=512)
                        for tb in range(ntb):
                            pg = a.pscnt % 4
                            a.pscnt += 1

                            def mm(h, wb=wb, tb=tb, pg=pg):
                                ins = None
                                for kc in range(KC):
                                    ins = h.matmul(psA[pg][:, :512], lhsT=a.hbuf[:, kc, tb * 128:(tb + 1) * 128],
                                                   rhs=a.wflat[:, wb[0] + kc * 512:wb[0] + (kc + 1) * 512], start=(kc == 0), stop=(kc == KC - 1))
                                return ins
                            P.op("pe", mm, r=wb[1] + [("h", kc) for kc in range(KC)], w=[("ps", pg)])
                            eng = "act" if tb % 2 == 0 else "dve"
                            if eng == "act":
                                P.op("act", lambda h, pg=pg, tb=tb, vst=vst: h.activation(out=vst[:, tb, :], in_=psA[pg][:, :512], func=AF.Copy),
                                     r=[("ps", pg)], w=[("hid", 4 * vb + i, 0) for i in range(4)] + [("hid", 4 * vb + i, 512) for i in range(4)])
                            else:
                                P.op("dve", lambda h, pg=pg, tb=tb, vst=vst: h.tensor_copy(out=vst[:, tb, :], in_=psA[pg][:, :512]),
                                     r=[("ps", pg)], w=[("hid", 4 * vb + i, 0) for i in range(4)] + [("hid", 4 * vb + i, 512) for i in range(4)])
                        P.op("sp", lambda h, vst=vst, dstD=dstD, colo=colo, ntb=ntb: h.dma_start(
                            out=dstD[c0:c0 + n, colo:colo + 512].rearrange("(tb p) c -> p tb c", p=128), in_=vst[:, :ntb, :]),
                            r=[("hid", 4 * vb + i, 0) for i in range(4)] + [("hid", 4 * vb + i, 512) for i in range(4)],
                            w=[("dv", blk, ti)], kind="dma_out", slot=("vstst", vb))
                for ti, tile in enumerate(TILES):
                    a_tile(ti, tile)
            P.barrier()

        def phase_b(l):
            last = (l == 1)
            with contextlib.ExitStack() as stB:
                def sbx(name, shape, dt=F32):
                    return stB.enter_context(nc.sbuf_tensor(uname(name), list(shape), dt))
                Sst = sbx("Sst", [128, 16, 128])
                Asum = sbx("Asum", [128, 16])
                lfb = sbx("lfb", [128, 2, 4, 256])
                kkb = sbx("kkb", [128, 2, 4, 256])
                qsb = sbx("qsb", [128, 2, 4, 256])
                bb = sbx("bb", [128, 4, 256])
                ub = sbx("ub", [128, 4, 256])
                dd = sbx("dd", [128, 4, 256])
                E2 = sbx("E2", [128, 4, 256])
                E3 = sbx("E3", [128, 4, 256])
                qm = sbx("qm", [128, 2, 4, 256], BF16)
                km = sbx("km", [128, 2, 4, 256], BF16)
                kmTok = sbx("kmTok", [64, 2, 16, 128], BF16)
                vt64 = sbx("vt64", [64, 2, 4, 512], BF16)
                attnT2 = sbx("attnT", [64, 2, 2, 16, 64], BF16)
                Smb = sbx("Smb", [128, 4, 2, 128], BF16)
                T1 = sbx("T1", [128, 4, 2, 128])
                gsa = sbx("gsa", [128, 2, 3, 16])
                gs = sbx("gs", [128, 2, 3, 16])
                bsum = sbx("bsum", [128, 4])
                osb = sbx("osb", [128, 4, 256])
                ofwb = sbx("ofwb", [128, 4, 256])
                sgb = sbx("sgb", [128, 4, 256])
                osq = sbx("osq", [128, 4, 256], BF16)
                rno = sbx("rno", [128, 4, 256])
                catst = sbx("catst", [128, 4, 256], BF16)
                smask = cst[:, C_SMASK:C_SMASK + 1024]
                cnt = [0]
                if deferred:
                    dstg = sbx("dstg", [128, 2, 4096])
                    dbf = sbx("dbf", [128, 2, 4096], BF16)
                dstate = {"n": 0, "prev": None}

                def emit_cast(npieces=1):
                    for _ in range(npieces):
                        if not deferred and dstate["prev"] is None:
                            return
                        n_ = dstate["n"]
                        bufi = n_ % 2
                        if deferred:
                            name, wl, wf, c0, cw = deferred.pop(0)
                            src_all = INP(name + "_t")[wl, wf]
                            P.op("sp", lambda h, bufi=bufi, c0=c0, cw=cw, src_all=src_all: h.dma_start(
                                out=dstg[:, bufi, :cw], in_=src_all[:, c0:c0 + cw]), w=[("dstg", bufi)], kind="dma_in", slot=("dstg", bufi))
                            P.op("pool", lambda h, bufi=bufi, cw=cw: h.tensor_copy(out=dbf[:, bufi, :cw], in_=dstg[:, bufi, :cw]),
                                 r=[("dstg", bufi)], w=[("dbf", bufi)])
                            cur = (name, wl, wf, c0, cw, bufi)
                        else:
                            cur = None
                        if dstate["prev"] is not None:
                            name, wl, wf, c0, cw, pb = dstate["prev"]
                            P.op("sp", lambda h, pb=pb, c0=c0, cw=cw, name=name, wl=wl, wf=wf: h.dma_start(
                                out=wgat[(name, wl, wf)][:, c0:c0 + cw], in_=dbf[:, pb, :cw]),
                                r=[("dbf", pb)], w=[("wl", name, wl, wf, c0)], kind="dma_out", slot=("dbfst", pb))
                        dstate["prev"] = cur
                        dstate["n"] = n_ + 1

                def hg_block(d, hg, t0, with_out, acc_A):
                    bi = cnt[0]
                    cnt[0] += 1
                    b = bi % 2
                    attnT = attnT2[:, d]
                    rows = slice(hg * 512, hg * 512 + 512)

                    def rowsv(Tn):
                        return Tn[rows, t0:t0 + 256].rearrange("(h p) t -> p h t", p=128)
                    P.op("sp", lambda h: h.dma_start(out=lfb[:, b], in_=rowsv(lfT[d])), w=[("lfb", b)], kind="dma_in", slot=("lfb", b))
                    P.op("sp", lambda h: h.dma_start(out=kkb[:, b], in_=rowsv(kT[d])), w=[("kkb", b)], kind="dma_in", slot=("kkb", b))
                    P.op("sp", lambda h: h.dma_start(out=vt64[:, b], in_=vtok[t0:t0 + 256, hg * 512:hg * 512 + 512].rearrange("(c p) f -> p c f", p=64)),
                         w=[("vt64", b)], kind="dma_in", slot=("vt64", b))
                    if with_out:
                        P.op("sp", lambda h: h.dma_start(out=qsb[:, b], in_=rowsv(qT)), w=[("qsb", b)], kind="dma_in", slot=("qsb", b))
                    yield "P"
                    P.op("dve", lambda h: h.tensor_tensor_scan(out=bb[:].rearrange("p h t -> p (h t)"), data0=smask,
                                                               data1=lfb[:, b].rearrange("p h t -> p (h t)"), initial=0.0,
                                                               op0=ALU.mult, op1=ALU.add), r=[("lfb", b), "cst"], w=["bb"])
                    if d == 0:
                        u = bb
                        mref = 31
                    else:
                        P.op("dve", lambda h: h.tensor_tensor(out=ub[:], in0=lfb[:, b], in1=bb[:], op=ALU.subtract), r=[("lfb", b), "bb"], w=["ub"])
                        u = ub
                        mref = 32
                    ukey = "bb" if d == 0 else "ub"
                    u4 = u[:].rearrange("p h (c t) -> p h c t", t=64)
                    b4 = bb[:].rearrange("p h (c t) -> p h c t", t=64)
                    um = u4[:, :, :, mref]
                    bt = b4[:, :, :, 63]
                    P.op("dve", lambda h: h.tensor_tensor(out=dd[:].rearrange("p h (c t) -> p h c t", t=64), in0=u4,
                                                          in1=u4[:, :, :, mref:mref + 1].to_broadcast([128, 4, 4, 64]), op=ALU.subtract),
                         r=[ukey], w=["dd"])
                    P.op("dve", lambda h: h.tensor_scalar(out=dd[:], in0=dd[:], scalar1=80.0, scalar2=-80.0, op0=ALU.min, op1=ALU.max), r=["dd"], w=["dd"])
                    yield "P"
                    ga = gsa[:, b].rearrange("p k (h c) -> p k h c", c=4)
                    if d == 0:
                        P.op("dve", lambda h: h.tensor_copy(out=ga[:, 0], in_=um), r=[ukey], w=[("gsa", b, 0)])
                        P.op("dve", lambda h: h.tensor_tensor(out=ga[:, 1], in0=bt, in1=um, op=ALU.subtract), r=[ukey, "bb"], w=[("gsa", b, 1)])
                    else:
                        P.op("dve", lambda h: h.tensor_tensor(out=ga[:, 0], in0=um, in1=bt, op=ALU.add), r=[ukey, "bb"], w=[("gsa", b, 0)])
                        P.op("dve", lambda h: h.tensor_scalar(out=ga[:, 1], in0=um, scalar1=-1.0, scalar2=None, op0=ALU.mult), r=[ukey], w=[("gsa", b, 1)])
                    P.op("dve", lambda h: h.tensor_copy(out=ga[:, 2], in_=bt), r=["bb"], w=[("gsa", b, 2)])
                    if acc_A:
                        P.op("dve", lambda h: h.tensor_reduce(out=bsum[:], in_=bt, axis=mybir.AxisListType.X, op=ALU.add), r=["bb"], w=["bsum"])
                        P.op("dve", lambda h: h.tensor_tensor(out=Asum[:, d * 8 + hg * 4:d * 8 + hg * 4 + 4], in0=Asum[:, d * 8 + hg * 4:d * 8 + hg * 4 + 4],
                                                              in1=bsum[:], op=ALU.add), r=["bsum", ("Asum", d, hg)], w=[("Asum", d, hg)])
                    P.op("act", lambda h: h.activation(out=gs[:, b], in_=gsa[:, b], func=AF.Exp),
                         r=[("gsa", b, 0), ("gsa", b, 1), ("gsa", b, 2)], w=[("gs", b)])
                    P.op("act", lambda h: h.activation(out=E3[:], in_=dd[:], func=AF.Exp, scale=-1.0), r=["dd"], w=["E3"])
                    P.op("dve", lambda h: h.tensor_tensor(out=km[:, b], in0=kkb[:, b], in1=E3[:], op=ALU.mult), r=[("kkb", b), "E3"], w=[("km", b)])
                    if with_out:
                        P.op("act", lambda h: h.activation(out=E2[:], in_=dd[:], func=AF.Exp), r=["dd"], w=["E2"])
                        P.op("dve", lambda h: h.tensor_tensor(out=qm[:, b], in0=qsb[:, b], in1=E2[:], op=ALU.mult), r=[("qsb", b), "E2"], w=[("qm", b)])
                    yield "P"
                    for rd in range(2):
                        def tr(h, rd=rd):
                            ins = None
                            for i in range(8):
                                hh = rd * 2 + i // 4
                                c = i % 4
                                ins = h.transpose(psB[0:64, i * 128:(i + 1) * 128], km[:, b, hh, c * 64:(c + 1) * 64], ident[:])
                            return ins
                        P.op("pe", tr, r=[("km", b), "ident"], w=["psB"])
                        P.op("act", lambda h, rd=rd: h.activation(out=kmTok[:, b, rd * 8:(rd + 1) * 8, :].rearrange("p i c -> p (i c)"),
                                                                    in_=psB[0:64, :], func=AF.Copy), r=["psB"], w=[("kmTok", b, rd)])
                    yield "P"
                    if with_out:
                        for rd in range(2):
                            def at(h, rd=rd):
                                ins = None
                                for i in range(8):
                                    hh = rd * 2 + i // 4
                                    c = i % 4
                                    if d == 0:
                                        h.matmul(psA[rd][0:32, i * 64:(i + 1) * 64], lhsT=km[:, b, hh, c * 64:c * 64 + 32],
                                                 rhs=qm[:, b, hh, c * 64:(c + 1) * 64], start=True, stop=True)
                                        ins = h.matmul(psA[rd][32:64, i * 64 + 32:(i + 1) * 64], lhsT=km[:, b, hh, c * 64 + 32:c * 64 + 64],
                                                       rhs=qm[:, b, hh, c * 64 + 32:(c + 1) * 64], start=True, stop=True)
                                    else:
                                        h.matmul(psA[rd][32:64, i * 64:(i + 1) * 64], lhsT=km[:, b, hh, c * 64 + 32:c * 64 + 64],
                                                 rhs=qm[:, b, hh, c * 64:(c + 1) * 64], start=True, stop=True)
                                        ins = h.matmul(psA[rd][0:32, i * 64:i * 64 + 32], lhsT=km[:, b, hh, c * 64:c * 64 + 32],
                                                       rhs=qm[:, b, hh, c * 64:c * 64 + 32], start=True, stop=True)
                                return ins
                            P.op("pe", at, r=[("km", b), ("qm", b)], w=[("psat", rd)])
                            pv = psA[rd][:, :].rearrange("p (i t) -> p i t", t=64)
                            av = attnT[:, b, rd * 8:(rd + 1) * 8, :]
                            if d == 0:
                                full_rows, part_rows, pc = slice(0, 32), slice(32, 64), slice(32, 64)
                            else:
                                full_rows, part_rows, pc = slice(32, 64), slice(0, 32), slice(0, 32)
                            P.op("dve", lambda h, pv=pv, av=av, fr=full_rows: h.tensor_tensor(
                                out=av[fr, :, :], in0=pv[fr, :, :], in1=trim[fr, d, :].unsqueeze(1).to_broadcast([32, 8, 64]), op=ALU.mult),
                                r=[("psat", rd), "trim0", "trim1"], w=[("attnT", d, b, rd)])
                            P.op("dve", lambda h, pv=pv, av=av, pr_=part_rows, pc=pc: h.tensor_tensor(
                                out=av[pr_, :, pc], in0=pv[pr_, :, pc], in1=trim[pr_, d, pc].unsqueeze(1).to_broadcast([32, 8, 32]), op=ALU.mult),
                                r=[("psat", rd), "trim0", "trim1"], w=[("attnT", d, b, rd)])
                    yield "L"
                    corder = range(4) if d == 0 else range(3, -1, -1)
                    for ci, c in enumerate(corder):
                        par = ci % 2
                        def pmm4(h, c=c, par=par):
                            ins = None
                            for hh in range(4):
                                ins = h.matmul(psA[2 + par][:, hh * 128:(hh + 1) * 128], lhsT=kmTok[0:64, b, hh * 4 + c, :],
                                               rhs=vt64[0:64, b, c, hh * 128:(hh + 1) * 128], start=True, stop=True)
                            return ins
                        P.op("pe", pmm4, r=[("kmTok", b, 0), ("kmTok", b, 1), ("vt64", b)], w=[("psP", par)])
                        for hh in range(4):
                            sidx = d * 8 + hg * 4 + hh
                            gi = hh * 4 + c
                            if with_out:
                                P.op("act", lambda h, hh=hh, par=par, sidx=sidx, gi=gi: h.activation(
                                    out=Smb[:, hh, par, :], in_=Sst[:, sidx, :], func=AF.Identity, scale=gs[:, b, 0, gi:gi + 1], bias=0.0),
                                    r=[("S", sidx), ("gs", b)], w=[("Smb", hh, par)])

                                def om(h, hh=hh, c=c, par=par):
                                    o_ap = psA[4 + hh // 2][:, (hh % 2) * 256 + c * 64:(hh % 2) * 256 + (c + 1) * 64]
                                    h.matmul(o_ap, lhsT=Smb[:, hh, par, :], rhs=qm[:, b, hh, c * 64:(c + 1) * 64], start=True, stop=False)
                                    return h.matmul(o_ap, lhsT=vt64[0:64, b, c, hh * 128:(hh + 1) * 128], rhs=attnT[0:64, b, hh * 4 + c, :],
                                                    start=False, stop=True)
                                P.op("pe", om, r=[("Smb", hh, par), ("qm", b), ("vt64", b), ("attnT", d, b, hh // 2)], w=[("pso", hh, c)])
                            P.op("act", lambda h, hh=hh, par=par, gi=gi: h.activation(
                                out=T1[:, hh, par, :], in_=psA[2 + par][:, hh * 128:(hh + 1) * 128], func=AF.Identity,
                                scale=gs[:, b, 1, gi:gi + 1], bias=0.0), r=[("psP", par), ("gs", b)], w=[("T1", hh, par)])
                            P.op("dve", lambda h, hh=hh, par=par, sidx=sidx, gi=gi: h.scalar_tensor_tensor(
                                out=Sst[:, sidx, :], in0=Sst[:, sidx, :], scalar=gs[:, b, 2, gi:gi + 1], in1=T1[:, hh, par, :],
                                op0=ALU.mult, op1=ALU.add), r=[("S", sidx), ("T1", hh, par), ("gs", b)], w=[("S", sidx)])
                        yield "C"
                    if not with_out:
                        return
                    okeys = [("pso", hh, c) for hh in range(4) for c in range(4)]
                    if d == 0:
                        P.op("act", lambda h: h.activation(out=osb[:, 0:2, :].rearrange("p h t -> p (h t)"), in_=psA[4][:, :], func=AF.Copy),
                             r=okeys[:8], w=[("osb", 0)])
                        P.op("dve", lambda h: h.tensor_copy(out=osb[:, 2:4, :].rearrange("p h t -> p (h t)"), in_=psA[5][:, :]),
                             r=okeys[8:], w=[("osb", 1)])
                        P.op("sp", lambda h: h.dma_start(out=rowsv(ofw), in_=osb[:]), r=[("osb", 0), ("osb", 1)],
                             w=[("dofw", hg, t0)], kind="dma_out", slot="osbst")
                    else:
                        P.op("sp", lambda h: h.dma_start(out=ofwb[:], in_=rowsv(ofw)), r=[("dofw", hg, t0)], w=["ofwb"], kind="dma_in", slot="ofwb")
                        P.op("sp", lambda h: h.dma_start(out=sgb[:], in_=rowsv(sgT)), w=["sgb"], kind="dma_in", slot="sgb")
                        for pr in range(2):
                            P.op("dve", lambda h, pr=pr: h.tensor_tensor(out=osb[:, 2 * pr:2 * pr + 2, :].rearrange("p h t -> p (h t)"),
                                                                         in0=psA[4 + pr][:, :], in1=ofwb[:, 2 * pr:2 * pr + 2, :].rearrange("p h t -> p (h t)"),
                                                                         op=ALU.add), r=okeys[8 * pr:8 * pr + 8] + ["ofwb"], w=[("osb", pr)])
                        P.op("act", lambda h: h.activation(out=osq[:], in_=osb[:], func=AF.Square), r=[("osb", 0), ("osb", 1)], w=["osq"])
                        for pr in range(2):
                            def nm(h, pr=pr):
                                h.matmul(psA[6][:, 0:256], lhsT=ones[:], rhs=osq[:, 2 * pr, :], start=True, stop=True)
                                return h.matmul(psA[6][:, 256:512], lhsT=ones[:], rhs=osq[:, 2 * pr + 1, :], start=True, stop=True)
                            P.op("pe", nm, r=["osq", "ones"], w=["psn"])
                            P.op("act", lambda h, pr=pr: h.activation(out=rno[:, 2 * pr:2 * pr + 2, :].rearrange("p h t -> p (h t)"), in_=psA[6][:, :],
                                                                        func=AF.Sqrt, scale=1.0 / 128, bias=eps_t[:, 0:1]), r=["psn", "eps"], w=[("rno", pr)])
                        P.op("dve", lambda h: h.reciprocal(out=rno[:], in_=rno[:]), r=[("rno", 0), ("rno", 1)], w=[("rno", 0), ("rno", 1)])
                        P.op("dve", lambda h: h.tensor_tensor(out=osb[:], in0=osb[:], in1=rno[:], op=ALU.mult),
                             r=[("osb", 0), ("osb", 1), ("rno", 0), ("rno", 1)], w=[("osb", 0), ("osb", 1)])
                        P.op("dve", lambda h: h.scalar_tensor_tensor(out=catst[:], in0=osb[:], scalar=hgain[:, l:l + 1], in1=sgb[:],
                                                                     op0=ALU.mult, op1=ALU.mult), r=[("osb", 0), ("osb", 1), "sgb", "hgain"], w=["catst"])
                        P.op("sp", lambda h: h.dma_start(out=rowsv(catT), in_=catst[:]), r=["catst"], w=[("dcat", hg, t0)], kind="dma_out", slot="catst")

                P.op("pool", lambda h: h.memset(attnT2[:], 0.0), w=[("attnT", d_, b_, rd_) for d_ in range(2) for b_ in range(2) for rd_ in range(2)])

                def zero_state():
                    P.op("pool", lambda h: h.memset(Sst[:], 0.0), r=[("S", i) for i in range(16)], w=[("S", i) for i in range(16)])

                zero_state()
                specs = []
                for d in range(2):
                    for hg in range(2):
                        specs.append((d, hg, TL, not last))
                import os as _os
                for d in range(2):
                    for hg in range(2):
                        blocks = range(TL // 256) if d == 0 else range(TL // 256 - 1, -1, -1)
                        if _os.environ.get("KDEBUG_NB") is not None:
                            nb_ = int(_os.environ["KDEBUG_NB"])
                            blocks = range(nb_) if d == 0 else range(nb_ - 1, -1, -1)
                        for bk in blocks:
                            specs.append((d, hg, bk * 256, True))
                gens = [hg_block(d_, hg_, t0_, wo_, False) for (d_, hg_, t0_, wo_) in specs]
                cur = gens[0]
                while next(cur) != "L":
                    pass
                for i_ in range(len(gens)):
                    nxt = gens[i_ + 1] if i_ + 1 < len(gens) else None
                    nxt_ready = False
                    while True:
                        try:
                            next(cur)
                        except StopIteration:
                            break
                        for _ in range(2):
                            if nxt is not None and not nxt_ready:
                                if next(nxt) == "L":
                                    nxt_ready = True
                    if nxt is not None:
                        while not nxt_ready:
                            if next(nxt) == "L":
                                nxt_ready = True
                    emit_cast()
                    cur = nxt
                while deferred or dstate["prev"] is not None:
                    emit_cast()
            P.barrier()

        def phase_c(l):
            last = (l == 1)
            ND = {0: (-1, 0), 1: (-1, 0, 1), 2: (-2, -1, 0, 1, 2), 3: tuple(range(-4, 5))}
            with contextlib.ExitStack() as stC:
                wpl = stC.enter_context(nc.sbuf_tensor(uname("wpl"), [128, 4, 2, 256], BF16))
                pm = stC.enter_context(nc.sbuf_tensor(uname("pm"), [128, NMAT, 128], BF16))
                P.op("sp", lambda h: h.dma_start(out=wpl[:], in_=wgat[("w_pool", l, 0)].rearrange("r x -> (r x)").rearrange(
                    "(p g c d) -> p g c d", p=128, g=4, c=2)), w=["wpl"], kind="dma_in", slot="wpl")
                with contextlib.ExitStack() as stH:
                    pmf = stH.enter_context(nc.sbuf_tensor(uname("pmf"), [128, NMAT, 128], F32))
                    P.op("sp", lambda h: h.dma_start(out=pmf[:], in_=INP("pmat")), w=["pmf"], kind="dma_in", slot="pmf")
                    P.op("dve", lambda h: h.tensor_copy(out=pm[:], in_=pmf[:]), r=["pmf"], w=["pm"])
                    P.barrier()
                P.op("pool", lambda h: h.memset(eps_t[:], EPS), w=["eps"])
                a = alloc_ac(stC, 16384)
                pext = a.ybuf[:, 0:8, :].bitcast(BF16).rearrange("p a (b c) -> p (a b) c", c=1024)
                dTv = a.ybuf[:, 8:12, :].bitcast(BF16).rearrange("p a (b c) -> p (a b) c", c=1024)
                invc = a.ybuf[:, 12:16, :]
                PEXT_K = [("y", i) for i in range(8)]
                DT_K = [("y", i) for i in range(8, 12)]
                INV_K = [("y", i) for i in range(12, 16)]
                wo_t2 = wtiles("w_out", l, 0, 2048)
                tiles = TILES if not last else TILES[:-1]
                def c_tile(ti, tile):
                    c0, n, wh = tile
                    sbs = subs_of(n)
                    off = 8 * ti - 4
                    P.op("sp", lambda h: h.dma_start(out=invc[:, :, :n], in_=INP("invcnt")[:, :, c0:c0 + n]), w=INV_K, kind="dma_in", slot="invc")
                    if wh == 0:
                        off = 8 * ti - 4
                        lo, hi = max(0, off), min(NBLK, off + 16)
                        P.op("sp", lambda h, lo=lo, hi=hi, off=off: h.dma_start(
                            out=pext[:, lo - off:hi - off, :], in_=ptok[lo * 128:hi * 128, :].rearrange("(b p) f -> p b f", p=128)),
                            w=PEXT_K, kind="dma_in", slot="pext")
                        groups = [(hb, [8 * ti + 4 * hb + o for o in range(4)]) for hb in range(2)]
                    else:
                        P.op("sp", lambda h: h.dma_start(out=pext[:, 0:2, :], in_=ptok[TL:T, :].rearrange("(b p) f -> p b f", p=128)),
                             w=PEXT_K, kind="dma_in", slot="pext")
                        groups = [(0, [0, 1])]
                    for fcg in range(8):
                        g = fcg // 2
                        for (hb, obs) in groups:
                            pg = a.pscnt % 4
                            a.pscnt += 1

                            def pmm(h, fcg=fcg, g=g, obs=obs, pg=pg):
                                ins = None
                                for oi, ob in enumerate(obs):
                                    if wh == 0:
                                        srcs = []
                                        for dl in ND[g]:
                                            eb = ob + dl
                                            if eb < 0 or eb >= NBLK:
                                                continue
                                            sap = pext[:, eb - off, fcg * 128:(fcg + 1) * 128]
                                            srcs.append((sap, _mat_for(midx, g, ob, dl)))
                                    else:
                                        srcs = [(pext[:, ib, fcg * 128:(fcg + 1) * 128], midx[("c", g, ob, ib)]) for ib in range(2)]
                                    for si_, (sap, mi) in enumerate(srcs):
                                        ins = h.matmul(psA[pg][:, oi * 128:(oi + 1) * 128], lhsT=sap, rhs=pm[:, mi, :],
                                                       start=(si_ == 0), stop=(si_ == len(srcs) - 1))
                                return ins
                            P.op("pe", pmm, r=PEXT_K, w=[("ps", pg)])
                            ncol = 128 * len(obs)
                            P.op("dve", lambda h, fcg=fcg, g=g, hb=hb, pg=pg, ncol=ncol: h.tensor_tensor(
                                out=dTv[:, fcg, hb * 512:hb * 512 + ncol], in0=psA[pg][:, :ncol], in1=invc[:, g, hb * 512:hb * 512 + ncol], op=ALU.mult),
                                r=[("ps", pg)] + INV_K, w=DT_K)
                    for dch in range(8):
                        g, dh = dch // 2, dch % 2
                        for (s0, sn) in sbs:
                            pg = a.pscnt % 4
                            a.pscnt += 1

                            def lmm(h, g=g, dh=dh, s0=s0, sn=sn, pg=pg):
                                h.matmul(psA[pg][:, :sn], lhsT=wpl[:, g, 0, dh * 128:(dh + 1) * 128], rhs=dTv[:, 2 * g, s0:s0 + sn], start=True, stop=False)
                                return h.matmul(psA[pg][:, :sn], lhsT=wpl[:, g, 1, dh * 128:(dh + 1) * 128], rhs=dTv[:, 2 * g + 1, s0:s0 + sn],
                                                start=False, stop=True)
                            P.op("pe", lmm, r=DT_K + ["wpl"], w=[("ps", pg)])
                            P.op("dve", lambda h, dch=dch, s0=s0, sn=sn, pg=pg: h.tensor_scalar(
                                out=a.hbuf[:, 8 + dch, s0:s0 + sn], in0=psA[pg][:, :sn], scalar1=bpool[:, l, dch:dch + 1],
                                scalar2=pscale[:, l, dch:dch + 1], op0=ALU.add, op1=ALU.mult),
                                r=[("ps", pg), "bpool", "pscale"], w=[("h", 8 + dch)])
                    if "dbg_pool" in dbg:
                        P.op("sp", lambda h: h.dma_start(out=dbg_pool_t[0][:, c0:c0 + n].rearrange("(k p) t -> p k t", p=128), in_=a.hbuf[:, 8:16, :n]),
                             r=[("h", 8 + i) for i in range(8)], w=[("dbgp", ti)], kind="dma_out", slot="dbgp")
                    P.op("sp", lambda h: h.dma_start(out=a.hbuf[:, 0:8, :n], in_=catT[0:1024, c0:c0 + n].rearrange("(k p) t -> p k t", p=128)),
                         w=[("h", i) for i in range(8)], kind="dma_in", slot="catld")
                    for m in range(KC):
                        wb = load_w(a, wo_t2[m], 2048, None)
                        for (s0, sn) in sbs:
                            pg = a.pscnt % 4
                            a.pscnt += 1

                            def omm(h, wb=wb, s0=s0, sn=sn, pg=pg):
                                ins = None
                                for kc in range(KC):
                                    ins = h.matmul(psA[pg][:, :sn], lhsT=a.wflat[:, wb[0] + kc * 128:wb[0] + (kc + 1) * 128],
                                                   rhs=a.hbuf[:, kc, s0:s0 + sn], start=(kc == 0), stop=(kc == KC - 1))
                                return ins
                            P.op("pe", omm, r=wb[1] + [("h", kc) for kc in range(KC)], w=[("ps", pg)])
                            P.op("act", lambda h, pg=pg, m=m, s0=s0, sn=sn: h.activation(out=a.ybuf[:, m, s0:s0 + sn], in_=psA[pg][:, :sn], func=AF.Copy),
                                 r=[("ps", pg)], w=[("y", m)])
                    residual(a, l, 1, tile, xs, xs, ti)
                    if last:
                        ffn(a, l, 1, tile, xs, outT, True, ti)
                    else:
                        ffn(a, l, 1, tile, xs, xs, True, ti)
                for ti, tile in enumerate(tiles):
                    c_tile(ti, tile)
            P.barrier()

        if mode == "bonly":
            phase_b(0)
            return done()
        for l in range(2):
            phase_a(l)
            if stop_after == ("a", l):
                return done()
            phase_b(l)
            if stop_after == ("b", l):
                return done()
            phase_c(l)
            if stop_after == ("c", l):
                return done()
        return done()


def make_in_maps(x, c, ctx, c_ctx, w_ada, b_ada, norm_gain, ffn_in, ffn_out, w_in, hgrn_lb, hgrn_gain,
                 w_pool, b_pool, pool_scale, w_out):
    f32 = np.float32
    x, c, ctx, c_ctx = (np.asarray(v, f32) for v in (x, c, ctx, c_ctx))
    w_ada, b_ada, norm_gain = (np.asarray(v, f32) for v in (w_ada, b_ada, norm_gain))
    ffn_in, ffn_out, w_in, w_out, w_pool = (np.asarray(v, f32) for v in (ffn_in, ffn_out, w_in, w_out, w_pool))
    hgrn_lb, hgrn_gain, b_pool, pool_scale = (np.asarray(v, f32) for v in (hgrn_lb, hgrn_gain, b_pool, pool_scale))
    tiled = _tile_weights(ffn_in, ffn_out, w_in, w_out, w_pool)
    wts = {}
    for name, E in WSPEC:
        nslot = 2 if name.startswith("ffn") else 1
        wts[name + "_t"] = np.ascontiguousarray(tiled[name].reshape(2, nslot, 128, E // 128))
    gainsT = np.ascontiguousarray(norm_gain.reshape(2, 6, KC, 128).transpose(3, 0, 1, 2))
    lbT = np.ascontiguousarray(hgrn_lb.reshape(2, 2, 8, 128).transpose(3, 0, 1, 2))
    hgainT = np.ascontiguousarray(hgrn_gain.T)
    bpoolT = np.ascontiguousarray(b_pool.reshape(2, 8, 128).transpose(2, 0, 1))
    pscaleT = np.ascontiguousarray(pool_scale.reshape(2, 8, 128).transpose(2, 0, 1))
    bada = np.ascontiguousarray(b_ada.reshape(2, 144, 128).transpose(2, 0, 1))
    wada = np.ascontiguousarray(w_ada)
    smask = np.ones(1024, f32)
    smask[::64] = 0.0
    ii = np.arange(64)
    consts = np.zeros((128, NCONST), f32)
    consts[:, C_ID:C_ID + 128] = np.eye(128, dtype=f32)
    consts[:64, C_TRIF:C_TRIF + 64] = (ii[:, None] <= ii[None, :]).astype(f32)
    consts[:64, C_TRIB:C_TRIB + 64] = (ii[:, None] >= ii[None, :]).astype(f32)
    consts[:, C_SMASK:C_SMASK + 1024] = smask[None, :]
    pmats, _, inv = _pool_tables()
    pmat = np.ascontiguousarray(pmats.transpose(1, 0, 2))
    invcnt = np.ascontiguousarray(np.broadcast_to(inv[None], (128, 4, T)))
    in_maps = []
    for b in range(NCORE):
        xT = np.ascontiguousarray(np.concatenate([x[b], ctx[b]], 0).T)
        ccT = np.ascontiguousarray(np.stack([c[b], c_ctx], 0).reshape(2, KC, 128).transpose(2, 1, 0))
        m = {"xT": xT, "ccT": ccT, "wada": wada, "bada": bada, "gainsT": gainsT, "lbT": lbT, "hgainT": hgainT,
             "bpoolT": bpoolT, "pscaleT": pscaleT, "consts": consts, "pmat": pmat, "invcnt": invcnt}
        m.update(wts)
        in_maps.append(m)
    return in_maps


_NC_CACHE = {}


def kernel(**inputs):
    in_maps = make_in_maps(**inputs)
    if "nc" not in _NC_CACHE:
        _NC_CACHE["nc"] = build_program()
    nc = _NC_CACHE["nc"]
    res = run_bass_kernel_spmd(nc, in_maps, core_ids=list(range(NCORE)))
    out = np.empty((NCORE, TL, D), np.float32)
    for b in range(NCORE):
        out[b] = res.results[b]["outT"].T
    return out
```

```python
import contextlib
import numpy as np
import concourse.bass as bass
import concourse.mybir as mybir
from concourse.bass_utils import run_bass_kernel_spmd

F32 = mybir.dt.float32
BF16 = mybir.dt.bfloat16
AF = mybir.ActivationFunctionType
ALU = mybir.AluOpType

D = 2048
KC = 16
DFF = 5632
JC = 44
TL = 16384
TCX = 256
T = TL + TCX
EPS = 1e-6
NCORE = 2
EPOCH = 8000
DLIM = 500
NBLK = TL // 128

C_SEL = 0
C_MFW = 2
C_MFWN = 10
C_MBW = 18
C_MBWN = 26
C_HTOP = 34
C_HBOT = 42
C_ID = 50
C_TRIF = 178
C_TRIB = 242
C_SMASK = 306
NCONST = 306 + 1024

POOL_W = (2, 4, 8, 16)
NTOP = (1, 1, 2, 4)
NBOT = (0, 1, 2, 4)


class Prog:
    ENGS = ("pe", "act", "dve", "pool", "sp")

    def __init__(self, nc, same_engine_sync=True):
        self.nc = nc
        self.ops = []
        self.last_w = {}
        self.readers = {}
        self.ses = same_engine_sync

    def op(self, eng, fn, r=(), w=(), kind="c", slot=None):
        if kind == "dma_out" and eng == "sp":
            eng = "pool"
        idx = len(self.ops)
        deps = set()
        for k in r:
            if k in self.last_w:
                deps.add(self.last_w[k])
        for k in w:
            if k in self.last_w:
                deps.add(self.last_w[k])
            for rd in self.readers.get(k, ()):
                deps.add(rd)
        for k in r:
            self.readers.setdefault(k, []).append(idx)
        for k in w:
            self.last_w[k] = idx
            self.readers[k] = []
        if kind != "c" and slot is None:
            slot = (w[0] if (len(w) and kind == "dma_in") else (r[0] if len(r) else w[0]))
        self.ops.append(dict(eng=eng, fn=fn, deps=deps, kind=kind, slot=slot))
        return idx

    def barrier(self):
        self.ops.append(dict(eng=None, kind="barrier", deps=set(), fn=None, slot=None))
        self.last_w = {}
        self.readers = {}

    def emit(self):
        nc = self.nc
        eng_cnt = {e: 0 for e in self.ENGS}
        slot_n = {}
        for o in self.ops:
            if o["kind"] == "c":
                eng_cnt[o["eng"]] += 1
                o["ticket"] = eng_cnt[o["eng"]]
            elif o["kind"] in ("dma_in", "dma_out"):
                s = o["slot"]
                k = slot_n.get(s, 0)
                slot_n[s] = k + 1
                o["sep"] = k // DLIM
                o["dval"] = 16 * (k % DLIM + 1)
            elif o["kind"] == "barrier":
                o["snap_eng"] = dict(eng_cnt)
                o["snap_slot"] = dict(slot_n)
        with contextlib.ExitStack() as st:
            psem = {}
            for e in self.ENGS:
                nep = eng_cnt[e] // EPOCH + 1
                psem[e] = [st.enter_context(nc.semaphore(f"p_{e}_{i}")) for i in range(nep)]
            ssem = {}
            for i, (s, n) in enumerate(slot_n.items()):
                for ep in range((n - 1) // DLIM + 1):
                    ssem[(s, ep)] = st.enter_context(nc.semaphore(f"d_{i}_{ep}"))
            self.nsem = sum(len(v) for v in psem.values()) + len(ssem)
            block = st.enter_context(nc.Block())
            ops = self.ops
            ses = self.ses

            def run_engine(e, h):
                waited_eng = {x: 0 for x in self.ENGS}
                waited_slot = {}

                def wait_ticket(src, tk):
                    if tk <= waited_eng[src]:
                        return
                    ep = (tk - 1) // EPOCH
                    h.wait_ge(psem[src][ep], tk - ep * EPOCH)
                    waited_eng[src] = tk

                def wait_slot(s, ep, v):
                    if (ep, v) <= waited_slot.get(s, (-1, 0)):
                        return
                    h.wait_ge(ssem[(s, ep)], v)
                    waited_slot[s] = (ep, v)

                for o in ops:
                    if o["kind"] == "barrier":
                        for src, tk in o["snap_eng"].items():
                            if src != e and tk > 0:
                                wait_ticket(src, tk)
                        for s, n in o["snap_slot"].items():
                            wait_slot(s, (n - 1) // DLIM, 16 * ((n - 1) % DLIM + 1))
                        continue
                    if o["eng"] != e:
                        continue
                    for di in sorted(o["deps"]):
                        dop = ops[di]
                        if dop["kind"] == "c":
                            if dop["eng"] == e:
                                if e == "pe" or e == "sp" or not ses:
                                    continue
                            wait_ticket(dop["eng"], dop["ticket"])
                        else:
                            wait_slot(dop["slot"], dop["sep"], dop["dval"])
                    ins = o["fn"](h)
                    if o["kind"] == "c":
                        tk = o["ticket"]
                        ep = (tk - 1) // EPOCH
                        ins.then_inc(psem[e][ep], 1)
                    else:
                        ins.then_inc(ssem[(o["slot"], o["sep"])], 16)
                if e == "sp":
                    for s, n in slot_n.items():
                        wait_slot(s, (n - 1) // DLIM, 16 * ((n - 1) % DLIM + 1))

            @block.sync
            def _(h):
                run_engine("sp", h)

            @block.tensor
            def _(h):
                run_engine("pe", h)

            @block.scalar
            def _(h):
                run_engine("act", h)

            @block.vector
            def _(h):
                run_engine("dve", h)

            @block.gpsimd
            def _(h):
                run_engine("pool", h)


def _pool_tables():
    mats = []
    idx = {}
    tt = np.arange(128)
    rl_t, c_t = tt // 64, tt % 64

    def add(key, m):
        idx[key] = len(mats)
        mats.append(m.astype(np.float32))

    for g, w in enumerate(POOL_W):
        half = w // 2
        nd = {1: (-1, 0), 2: (-1, 0, 1), 4: (-2, -1, 0, 1, 2), 8: tuple(range(-4, 5))}[half]
        for dl in nd:
            rs = (2 * dl + rl_t)[:, None]
            rt = rl_t[None, :]
            cs = c_t[:, None]
            ct = c_t[None, :]
            m = ((rs >= rt - half) & (rs < rt + half) & (cs >= ct - half) & (cs < ct + half)).astype(np.float32)
            if dl != 0:
                add(("g", g, dl), m)
            else:
                classes = [("mid", 8)] + [("top", k) for k in range(NTOP[g])] + [("bot", k) for k in range(NBOT[g])]
                for cls, k in classes:
                    ob = 8 if cls == "mid" else (k if cls == "top" else NBLK - NBOT[g] + k)
                    rg = 2 * ob + rl_t
                    cnt_r = np.minimum(256, rg + half) - np.maximum(0, rg - half)
                    cnt_c = np.minimum(64, c_t + half) - np.maximum(0, c_t - half)
                    mm = m.copy()
                    mm[tt, tt] -= (cnt_r * cnt_c)
                    add(("d", g, cls, k), mm)
    for g, w in enumerate(POOL_W):
        half = w // 2
        for ob in range(2):
            for ib in range(2):
                sidx = (128 * ib + tt)[:, None]
                t = (128 * ob + tt)[None, :]
                m = ((sidx >= t - half) & (sidx < t + half)).astype(np.float32)
                if ib == ob:
                    tg = 128 * ob + tt
                    cnt = np.minimum(256, tg + half) - np.maximum(0, tg - half)
                    m[tt, tt] -= cnt
                add(("c", g, ob, ib), m)
    pm = np.stack(mats, 0)
    inv = np.zeros((4, T), np.float32)
    tl = np.arange(TL)
    rg = tl // 64
    cc = tl % 64
    tcx = np.arange(TCX)
    for g, w in enumerate(POOL_W):
        half = w // 2
        cnt_r = np.minimum(256, rg + half) - np.maximum(0, rg - half)
        cnt_c = np.minimum(64, cc + half) - np.maximum(0, cc - half)
        inv[g, :TL] = 1.0 / (cnt_r * cnt_c)
        cnt = np.minimum(256, tcx + half) - np.maximum(0, tcx - half)
        inv[g, TL:] = 1.0 / cnt
    return pm, idx, inv


def _pool_mat_index():
    _, idx, _ = _pool_tables()
    return idx


def _mat_for(idx, g, ob, dl):
    if dl != 0:
        return idx[("g", g, dl)]
    if ob < NTOP[g]:
        return idx[("d", g, "top", ob)]
    if ob >= NBLK - NBOT[g]:
        return idx[("d", g, "bot", ob - (NBLK - NBOT[g]))]
    return idx[("d", g, "mid", 8)]


def _tile_weights(ffn_in, ffn_out, w_in, w_out, w_pool):
    out = {}
    L = ffn_in.shape[0]
    fi = np.empty((L, 2, JC, 128, KC, 256), np.float32)
    fo = np.empty((L, 2, KC, 128, JC, 128), np.float32)
    for l in range(L):
        for f in range(2):
            W = ffn_in[l, f].reshape(KC, 128, 2, JC, 128)
            fi[l, f] = W.transpose(3, 1, 0, 2, 4).reshape(JC, 128, KC, 256)
            W2 = ffn_out[l, f].reshape(JC, 128, KC, 128)
            fo[l, f] = W2.transpose(2, 1, 0, 3)
    out["ffn_in"] = fi.reshape(L, 2, -1)
    out["ffn_out"] = fo.reshape(L, 2, -1)
    wi = np.empty((L, D * 6144), np.float32)
    wo = np.empty((L, KC, 128, KC, 128), np.float32)
    fm_src = list(range(0, 24)) + list(range(32, 40))
    for l in range(L):
        W = w_in[l].reshape(KC, 128, 48, 128)
        fm = W[:, :, fm_src, :].transpose(2, 1, 0, 3)
        tmb = []
        for c0 in (24, 28, 40, 44):
            blk = W[:, :, c0:c0 + 4, :].reshape(KC, 128, 512).transpose(1, 0, 2)
            tmb.append(blk.reshape(-1))
        wi[l] = np.concatenate([fm.reshape(-1)] + tmb)
        W3 = w_out[l].reshape(KC, 128, KC, 128)
        wo[l] = W3.transpose(2, 1, 0, 3)
    out["w_in"] = wi
    out["w_out"] = wo.reshape(L, -1)
    wp = w_pool.reshape(L, 4, 2, 128, 256).transpose(0, 3, 1, 2, 4)
    out["w_pool"] = np.ascontiguousarray(wp).reshape(L, -1)
    return out


WSPEC = [
    ("ffn_in", D * 2 * DFF),
    ("ffn_out", DFF * D),
    ("w_in", D * 6144),
    ("w_out", D * D),
    ("w_pool", 4 * 256 * 256),
]


def build_program(dbg=(), stop_after=None, ses=False, ext_in=(), mode=None):
    nc = bass.Bass("TRN2", target_bir_lowering=False)
    P = Prog(nc, same_engine_sync=ses)
    midx = _pool_mat_index()
    NMAT = len(midx)

    def din(name, shape, dt=F32):
        return nc.dram_tensor(name, list(shape), dt, kind="ExternalInput").ap()

    def dscr(name, shape, dt=F32):
        if name in ext_in:
            return nc.dram_tensor(name, list(shape), dt, kind="ExternalInput").ap()
        if name in dbg:
            return nc.dram_tensor(name, list(shape), dt, kind="ExternalOutput").ap()
        return nc.dram_tensor(name, list(shape), dt).ap()

    _inp = {}
    _shapes = {
        "xT": [D, T], "ccT": [128, KC, 2], "wada": [2, D, 9 * D], "bada": [128, 2, 144], "gainsT": [128, 2, 6, KC],
        "lbT": [128, 2, 2, 8], "hgainT": [128, 2], "bpoolT": [128, 2, 8], "pscaleT": [128, 2, 8],
        "consts": [128, NCONST], "pmat": [128, NMAT, 128], "invcnt": [128, 4, T],
    }
    for name, E in WSPEC:
        nslot = 2 if name.startswith("ffn") else 1
        _shapes[name + "_t"] = [2, nslot, 128, E // 128]

    def INP(name):
        if name not in _inp:
            _inp[name] = din(name, _shapes[name])
        return _inp[name]

    outT = nc.dram_tensor("outT", [D, TL], F32, kind="ExternalOutput").ap()

    xs = dscr("xs", [D, T])
    qT = dscr("qT", [1024, T])
    sgT = dscr("sgT", [1024, T])
    lfT = [dscr(f"lfT{d}", [1024, T]) for d in range(2)]
    kT = [dscr(f"kT{d}", [1024, T]) for d in range(2)]
    vtok = dscr("vtok", [T, 1024], BF16)
    ptok = dscr("ptok", [T, 1024], BF16)
    ofw = dscr("ofw", [1024, T])
    catT = dscr("catT", [D, T], BF16)
    dbg_pool_t = [dscr("dbg_pool", [1024, T], BF16)] if "dbg_pool" in dbg else []
    wgat = {}
    for name, E in WSPEC:
        nslot = 2 if name.startswith("ffn") else 1
        for l in range(2):
            for f in range(nslot):
                wgat[(name, l, f)] = nc.dram_tensor(f"wb_{name}_{l}_{f}", [128, E // 128], BF16).ap()

    def wtiles(name, l, f, per_part):
        return wgat[(name, l, f)].rearrange("r x -> (r x)").rearrange("(t p x) -> t p x", p=128, x=per_part)

    es = contextlib.ExitStack()
    _uid = [0]

    def uname(name):
        _uid[0] += 1
        return f"s{_uid[0]}_{name}"

    def done():
        P.emit()
        return nc

    with es:
        def sb(name, shape, dt=F32):
            return es.enter_context(nc.sbuf_tensor(uname(name), list(shape), dt))

        def pst(name, shape, dt=F32):
            return es.enter_context(nc.psum_tensor(uname(name), list(shape), dt))

        cst = sb("cst", [128, NCONST])
        ident = sb("ident", [128, 128], BF16)
        ones = sb("ones", [128, 128], BF16)
        trim = sb("trim", [64, 2, 64])
        gains = sb("gains", [128, 2, 6, KC])
        oml = sb("oml", [128, 2, 2, 8])
        hgain = sb("hgain", [128, 2])
        bpool = sb("bpool", [128, 2, 8])
        pscale = sb("pscale", [128, 2, 8])
        Mx = sb("Mx", [128, 2, 144])
        Mc = sb("Mc", [128, 2, 144])
        tabs = sb("tabs", [128, 2, 3, 2, 2, KC])
        eps_t = sb("eps_t", [128, 1])
        psA = [pst(f"psA{i}", [128, 512]) for i in range(7)]
        psB = pst("psB", [128, 1024], BF16)

        deferred = []
        for i_, (dst_, nm_) in enumerate(((cst, "consts"), (gains, "gainsT"), (hgain, "hgainT"), (bpool, "bpoolT"), (pscale, "pscaleT"))):
            P.op("sp", lambda h, dst_=dst_, nm_=nm_: h.dma_start(out=dst_[:], in_=INP(nm_)), w=[nm_], kind="dma_in", slot=("ldc", i_))
        P.op("pool", lambda h: h.memset(ones[:], 1.0), w=["ones"])
        P.op("pool", lambda h: h.memset(eps_t[:], EPS), w=["eps"])
        P.op("dve", lambda h: h.tensor_copy(out=ident[:], in_=cst[:, C_ID:C_ID + 128]), r=["consts"], w=["ident"])
        P.op("dve", lambda h: h.tensor_copy(out=trim[:, 0, :], in_=cst[0:64, C_TRIF:C_TRIF + 64]), r=["consts"], w=["trim0"])
        P.op("dve", lambda h: h.tensor_copy(out=trim[:, 1, :], in_=cst[0:64, C_TRIB:C_TRIB + 64]), r=["consts"], w=["trim1"])

        if mode != "bonly":
            with contextlib.ExitStack() as ps0:
                def sb0(name, shape, dt=F32):
                    return ps0.enter_context(nc.sbuf_tensor(uname(name), list(shape), dt))
                cc = sb0("cc", [128, KC, 2])
                scc = sb0("scc", [128, KC, 2])
                wad = sb0("wad", [128, 2, KC, 512])
                bada = sb0("bada", [128, 2, 144])
                lbt = sb0("lbt", [128, 2, 2, 8])
                lbd = sb0("lbd", [128, 2, 8])
                cstg = sb0("cstg", [128, 2, 4096])
                cbf = sb0("cbf", [128, 2, 4096], BF16)

                P.op("sp", lambda h: h.dma_start(out=lbt[:], in_=INP("lbT")), w=["lbt"], kind="dma_in", slot="ld_lb")
                P.op("dve", lambda h: h.tensor_tensor(out=lbd[:], in0=lbt[:, 1], in1=lbt[:, 0], op=ALU.subtract), r=["lbt"], w=["lbd"])
                P.op("act", lambda h: h.activation(out=lbd[:], in_=lbd[:], func=AF.Sigmoid), r=["lbd"], w=["lbd"])
                P.op("pool", lambda h: h.memset(oml[:, 0], 1.0), w=["oml0"])
                P.op("dve", lambda h: h.tensor_scalar(out=oml[:, 1], in0=lbd[:], scalar1=-1.0, scalar2=1.0, op0=ALU.mult, op1=ALU.add),
                     r=["lbd"], w=["oml1"])

                P.op("sp", lambda h: h.dma_start(out=cc[:], in_=INP("ccT")), w=["cc"], kind="dma_in", slot="ld_cc")
                P.op("sp", lambda h: h.dma_start(out=bada[:], in_=INP("bada")), w=["bada"], kind="dma_in", slot="ld_bada")
                P.op("act", lambda h: h.activation(out=scc[:], in_=cc[:], func=AF.Silu), r=["cc"], w=["scc"])
                for l in range(2):
                    for pc in range(36):
                        bufi = (l * 36 + pc) % 2
                        src = INP("wada")[l].rearrange("(kc p) n -> p kc n", p=128)[:, :, pc * 512:(pc + 1) * 512]
                        P.op("sp", lambda h, bufi=bufi, src=src: h.dma_start(out=wad[:, bufi], in_=src),
                             w=[("wad", bufi)], kind="dma_in", slot=("wad", bufi))

                        def mm(h, l=l, pc=pc, bufi=bufi):
                            ins = None
                            for c4 in range(4):
                                c = pc * 4 + c4
                                for kc in range(KC):
                                    ins = h.matmul(psA[5 + l][:, c * 2:c * 2 + 2],
                                                   lhsT=wad[:, bufi, kc, c4 * 128:(c4 + 1) * 128], rhs=scc[:, kc, :],
                                                   start=(kc == 0), stop=(kc == KC - 1))
                            return ins
                        P.op("pe", mm, r=[("wad", bufi), "scc"], w=[("ps", 5 + l)])
                for l in range(2):
                    pv = psA[5 + l][:, 0:288].rearrange("p (c b) -> p c b", b=2)
                    P.op("dve", lambda h, l=l, pv=pv: h.tensor_tensor(out=Mx[:, l, :], in0=pv[:, :, 0], in1=bada[:, l, :], op=ALU.add),
                         r=[("ps", 5 + l), "bada"], w=[("Mx", l)])
                    P.op("dve", lambda h, l=l, pv=pv: h.tensor_tensor(out=Mc[:, l, :], in0=pv[:, :, 1], in1=bada[:, l, :], op=ALU.add),
                         r=[("ps", 5 + l), "bada"], w=[("Mc", l)])
                allM = [("Mx", l) for l in range(2)] + [("Mc", l) for l in range(2)]
                for l in range(2):
                    for sub in range(3):
                        n0 = 3 * sub
                        gi, go = 2 * sub, 2 * sub + 1
                        cf = 1.0 if sub == 1 else 0.5
                        for wh, M in enumerate((Mx, Mc)):
                            scale = M[:, l, (n0 + 1) * 16:(n0 + 2) * 16]
                            gate = M[:, l, (n0 + 2) * 16:(n0 + 3) * 16]
                            P.op("dve", lambda h, l=l, sub=sub, wh=wh, scale=scale, gi=gi: h.scalar_tensor_tensor(
                                out=tabs[:, l, sub, wh, 0, :], in0=scale, scalar=1.0, in1=gains[:, l, gi, :], op0=ALU.add, op1=ALU.mult),
                                r=allM + ["gainsT"], w=[("tabs", l, sub, wh, 0)])
                            P.op("dve", lambda h, l=l, sub=sub, wh=wh, gate=gate, go=go, cf=cf: h.scalar_tensor_tensor(
                                out=tabs[:, l, sub, wh, 1, :], in0=gate, scalar=cf, in1=gains[:, l, go, :], op0=ALU.mult, op1=ALU.mult),
                                r=allM + ["gainsT"], w=[("tabs", l, sub, wh, 1)])
                if stop_after == "mod":
                    dbg_tabs = nc.dram_tensor("dbg_tabs", [128, 2 * 3 * 2 * 2 * KC], F32, kind="ExternalOutput").ap()
                    P.op("sp", lambda h: h.dma_start(out=dbg_tabs, in_=tabs[:].rearrange("p a b c d e -> p (a b c d e)")),
                         r=[("tabs", l, sub, wh, i) for l in range(2) for sub in range(3) for wh in range(2) for i in range(2)],
                         w=["dbgt"], kind="dma_out", slot="dbgt")
                    return done()

                order = []
                for l in range(2):
                    order += [("ffn_in", l, 0), ("ffn_out", l, 0), ("w_in", l, 0), ("w_pool", l, 0), ("w_out", l, 0),
                              ("ffn_in", l, 1), ("ffn_out", l, 1)]
                import os as _os
                if _os.environ.get("KDEBUG_NW"):
                    order = order[:int(_os.environ["KDEBUG_NW"])]
                pcs = 0
                if not _os.environ.get("KDEBUG_NW"):
                    for (name, l, f) in order[3:]:
                        Fw = dict(WSPEC)[name] // 128
                        for c0 in range(0, Fw, 4096):
                            deferred.append((name, l, f, c0, min(4096, Fw - c0)))
                    order = order[:3]
                for (name, l, f) in order:
                    Fw = dict(WSPEC)[name] // 128
                    src_all = INP(name + "_t")[l, f]
                    for c0 in range(0, Fw, 4096):
                        cw = min(4096, Fw - c0)
                        bufi = pcs % 2
                        eng = ("dve", "act", "pool")[pcs % 3]
                        pcs += 1
                        P.op("sp", lambda h, bufi=bufi, c0=c0, cw=cw, src_all=src_all: h.dma_start(
                            out=cstg[:, bufi, :cw], in_=src_all[:, c0:c0 + cw]), w=[("cstg", bufi)], kind="dma_in", slot=("cstg", bufi))
                        if eng == "act":
                            P.op("act", lambda h, bufi=bufi, cw=cw: h.activation(out=cbf[:, bufi, :cw], in_=cstg[:, bufi, :cw], func=AF.Copy),
                                 r=[("cstg", bufi)], w=[("cbf", bufi)])
                        else:
                            P.op(eng, lambda h, bufi=bufi, cw=cw: h.tensor_copy(out=cbf[:, bufi, :cw], in_=cstg[:, bufi, :cw]),
                                 r=[("cstg", bufi)], w=[("cbf", bufi)])
                        P.op("sp", lambda h, bufi=bufi, c0=c0, cw=cw, name=name, l=l, f=f: h.dma_start(
                            out=wgat[(name, l, f)][:, c0:c0 + cw], in_=cbf[:, bufi, :cw]),
                            r=[("cbf", bufi)], w=[("wl", name, l, f, c0)], kind="dma_out", slot=("cbfst", bufi))
        P.barrier()
        if "tabs" in dbg:
            dbg_w = nc.dram_tensor("dbg_w", [128, KC * 256], BF16, kind="ExternalOutput").ap()
            P.op("sp", lambda h: h.dma_start(out=dbg_w, in_=wtiles("ffn_in", 0, 0, KC * 256)[3]), w=["dbgw"], kind="dma_out", slot="dbgw")
        if stop_after == "prologue":
            return done()

        TILES = [(1024 * i, 1024, 0) for i in range(TL // 1024)] + [(TL, TCX, 1)]

        def subs_of(n):
            return [(0, 512), (512, 512)] if n == 1024 else [(0, n)]

        class AC:
            pass

        def alloc_ac(stack, wbuf_cols):
            a = AC()

            def sbx(name, shape, dt=F32):
                return stack.enter_context(nc.sbuf_tensor(uname(name), list(shape), dt))
            a.hbuf = sbx("hbuf", [128, KC, 1024], BF16)
            a.hid = sbx("hid", [128, 11, 1024], BF16)
            a.ybuf = sbx("ybuf", [128, KC, 1024])
            a.xst = sbx("xst", [128, 2, 1024])
            a.sq = sbx("sq", [128, 2, 1024], BF16)
            a.rstd = sbx("rstd", [128, 1024])
            a.tmp = sbx("tmp", [128, 2, 1024])
            a.wflat = sbx("wbuf", [128, wbuf_cols], BF16)
            a.nslots = wbuf_cols // 2048
            a.wcnt = 0
            a.pscnt = 0
            a.tcnt = 0
            a.xcnt = 0
            a.ycnt = 0
            return a

        def rms_stats(a, n, srcs, tag):
            sbs = subs_of(n)
            for kc in range(KC):
                b = kc % 2
                P.op("act", lambda h, kc=kc, b=b: h.activation(out=a.sq[:, b, :n], in_=a.ybuf[:, kc, :n], func=AF.Square),
                     r=[("y", kc)], w=[("sq", b)])

                def mm(h, kc=kc, b=b):
                    ins = None
                    for si, (s0, sn) in enumerate(sbs):
                        ins = h.matmul(psA[5 + si][:, :sn], lhsT=ones[:], rhs=a.sq[:, b, s0:s0 + sn],
                                       start=(kc == 0), stop=(kc == KC - 1))
                    return ins
                P.op("pe", mm, r=[("sq", b), "ones"], w=[("ps", 5), ("ps", 6)])
            for si, (s0, sn) in enumerate(sbs):
                P.op("act", lambda h, si=si, s0=s0, sn=sn: h.activation(out=a.rstd[:, s0:s0 + sn], in_=psA[5 + si][:, :sn],
                                                                      func=AF.Sqrt, scale=1.0 / D, bias=eps_t[:, 0:1]),
                     r=[("ps", 5), ("ps", 6), "eps"], w=[("rstd", si)])
                P.op("dve", lambda h, s0=s0, sn=sn: h.reciprocal(out=a.rstd[:, s0:s0 + sn], in_=a.rstd[:, s0:s0 + sn]),
                     r=[("rstd", si)], w=[("rstd", si)])

        def modulate_to_h(a, n, l, sub, wh):
            M = Mx if wh == 0 else Mc
            for kc in range(KC):
                b = a.tcnt % 2
                a.tcnt += 1
                P.op("dve", lambda h, kc=kc, b=b: h.tensor_tensor(out=a.tmp[:, b, :n], in0=a.ybuf[:, kc, :n], in1=a.rstd[:, :n], op=ALU.mult),
                     r=[("y", kc), ("rstd", 0), ("rstd", 1)], w=[("tmp", b)])
                P.op("act", lambda h, kc=kc, b=b: h.activation(
                    out=a.hbuf[:, kc, :n], in_=a.tmp[:, b, :n], func=AF.Identity,
                    scale=tabs[:, l, sub, wh, 0, kc:kc + 1], bias=M[:, l, 3 * sub * 16 + kc:3 * sub * 16 + kc + 1]),
                    r=[("tmp", b)], w=[("h", kc)])

        WSLOT = 2048

        def load_w(a, src_ap, ncols, wkey):
            k = (ncols + WSLOT - 1) // WSLOT
            if a.wcnt + k > a.nslots:
                a.wcnt = 0
            s0 = a.wcnt
            a.wcnt += k
            keys = [("w", s0 + i) for i in range(k)]
            off = s0 * WSLOT
            P.op("sp", lambda h, off=off: h.dma_start(out=a.wflat[:, off:off + ncols], in_=src_ap), w=keys, kind="dma_in", slot=("w", s0))
            return (off, keys)

        def ffn(a, l, f, tile, src, dst, x_in_y, ti):
            c0, n, wh = tile
            sbs = subs_of(n)
            sub = 0 if f == 0 else 2
            if not x_in_y:
                for g4 in range(4):
                    P.op("sp", lambda h, g4=g4: h.dma_start(
                        out=a.ybuf[:, 4 * g4:4 * g4 + 4, :n],
                        in_=src[512 * g4:512 * g4 + 512, c0:c0 + n].rearrange("(k p) t -> p k t", p=128)),
                        w=[("y", 4 * g4 + i) for i in range(4)], kind="dma_in", slot=("yld", g4))
            rms_stats(a, n, None, "pre")
            modulate_to_h(a, n, l, sub, wh)
            wi_t = wtiles("ffn_in", l, f, KC * 256)
            wo_t = wtiles("ffn_out", l, f, JC * 128)
            for q4 in range(4):
                for jj in range(11):
                    j = q4 * 11 + jj
                    wb = load_w(a, wi_t[j], KC * 256, ("wg", "ffn_in", l, f))
                    for (s0, sn) in sbs:
                        pg = (a.pscnt % 2) * 2
                        a.pscnt += 1

                        def mm(h, wb=wb, s0=s0, sn=sn, pg=pg):
                            ins = None
                            for u in range(2):
                                for kc in range(KC):
                                    ins = h.matmul(psA[pg + u][:, :sn], lhsT=a.wflat[:, wb[0] + kc * 256 + u * 128:wb[0] + kc * 256 + u * 128 + 128],
                                                   rhs=a.hbuf[:, kc, s0:s0 + sn], start=(kc == 0), stop=(kc == KC - 1))
                            return ins
                        P.op("pe", mm, r=wb[1] + [("h", kc) for kc in range(KC)], w=[("ps", pg), ("ps", pg + 1)])
                        b = a.tcnt % 2
                        a.tcnt += 1
                        P.op("act", lambda h, pg=pg, sn=sn, b=b: h.activation(out=a.tmp[:, b, :sn], in_=psA[pg][:, :sn], func=AF.Silu),
                             r=[("ps", pg)], w=[("tmp", b)])
                        P.op("dve", lambda h, pg=pg, sn=sn, b=b, jj=jj, s0=s0: h.tensor_tensor(
                            out=a.hid[:, jj, s0:s0 + sn], in0=a.tmp[:, b, :sn], in1=psA[pg + 1][:, :sn], op=ALU.mult),
                            r=[("tmp", b), ("ps", pg + 1)], w=[("hid", jj, s0)])
                for m in range(KC):
                    wb = load_w(a, wo_t[m][:, q4 * 11 * 128:(q4 + 1) * 11 * 128], 11 * 128, ("wg", "ffn_out", l, f))
                    for (s0, sn) in sbs:
                        pg = 4 + 0 * (a.pscnt % 2)
                        pg = (a.pscnt % 2) * 2 + (a.pscnt // 2) % 2
                        a.pscnt += 1

                        def mm2(h, wb=wb, s0=s0, sn=sn, pg=pg):
                            ins = None
                            for jj in range(11):
                                ins = h.matmul(psA[pg][:, :sn], lhsT=a.wflat[:, wb[0] + jj * 128:wb[0] + (jj + 1) * 128],
                                               rhs=a.hid[:, jj, s0:s0 + sn], start=(jj == 0), stop=(jj == 10))
                            return ins
                        P.op("pe", mm2, r=wb[1] + [("hid", jj, s0) for jj in range(11)], w=[("ps", pg)])
                        if q4 == 0:
                            P.op("act", lambda h, pg=pg, m=m, s0=s0, sn=sn: h.activation(out=a.ybuf[:, m, s0:s0 + sn], in_=psA[pg][:, :sn], func=AF.Copy),
                                 r=[("ps", pg)], w=[("y", m)])
                        else:
                            P.op("dve", lambda h, pg=pg, m=m, s0=s0, sn=sn: h.tensor_tensor(
                                out=a.ybuf[:, m, s0:s0 + sn], in0=a.ybuf[:, m, s0:s0 + sn], in1=psA[pg][:, :sn], op=ALU.add),
                                r=[("ps", pg), ("y", m)], w=[("y", m)])
            residual(a, l, sub, tile, src, dst, ti)

        def residual(a, l, sub, tile, src, dst, ti, dst_cols=None):
            c0, n, wh = tile
            rms_stats(a, n, None, "post")
            for m in range(KC):
                xb = a.xcnt % 2
                a.xcnt += 1
                P.op("sp", lambda h, m=m, xb=xb: h.dma_start(out=a.xst[:, xb, :n], in_=src[m * 128:(m + 1) * 128, c0:c0 + n]),
                     r=[("dx", ti, m)], w=[("xst", xb)], kind="dma_in", slot=("xst", xb))
                b = a.tcnt % 2
                a.tcnt += 1
                P.op("dve", lambda h, m=m, b=b: h.tensor_tensor(out=a.tmp[:, b, :n], in0=a.ybuf[:, m, :n], in1=a.rstd[:, :n], op=ALU.mult),
                     r=[("y", m), ("rstd", 0), ("rstd", 1)], w=[("tmp", b)])
                P.op("dve", lambda h, m=m, b=b, xb=xb: h.scalar_tensor_tensor(
                    out=a.ybuf[:, m, :n], in0=a.tmp[:, b, :n], scalar=tabs[:, l, sub, wh, 1, m:m + 1], in1=a.xst[:, xb, :n],
                    op0=ALU.mult, op1=ALU.add), r=[("tmp", b), ("xst", xb)], w=[("y", m)])
                if dst is not None:
                    dc0 = c0 if dst_cols is None else dst_cols
                    P.op("sp", lambda h, m=m, dc0=dc0: h.dma_start(out=dst[m * 128:(m + 1) * 128, dc0:dc0 + n], in_=a.ybuf[:, m, :n]),
                         r=[("y", m)], w=[("dx", ti, m)], kind="dma_out", slot=("yst", m % 4))

        def phase_a(l):
            with contextlib.ExitStack() as stA:
                a = alloc_ac(stA, 24576)
                src = INP("xT") if l == 0 else xs
                w_fm = wgat[("w_in", l, 0)].rearrange("r x -> (r x)")[0:32 * 128 * 2048].rearrange("(t p x) -> t p x", p=128, x=2048)
                w_tm = wgat[("w_in", l, 0)].rearrange("r x -> (r x)")[32 * 128 * 2048:].rearrange("(t p x) -> t p x", p=128, x=8192)
                def a_tile(ti, tile):
                    c0, n, wh = tile
                    sbs = subs_of(n)
                    ffn(a, l, 0, tile, src, xs, False, ti)
                    rms_stats(a, n, None, "mix")
                    modulate_to_h(a, n, l, 1, wh)
                    for fc in range(32):
                        typ = fc // 8
                        hh = fc % 8
                        wb = load_w(a, w_fm[fc], 2048, ("wg", "w_in", l, 0))
                        ob = a.ycnt % 16
                        a.ycnt += 1
                        ob2 = a.ycnt % 16
                        for (s0, sn) in sbs:
                            pg = a.pscnt % 4
                            a.pscnt += 1

                            def mm(h, wb=wb, s0=s0, sn=sn, pg=pg):
                                ins = None
                                for kc in range(KC):
                                    ins = h.matmul(psA[pg][:, :sn], lhsT=a.wflat[:, wb[0] + kc * 128:wb[0] + (kc + 1) * 128],
                                                   rhs=a.hbuf[:, kc, s0:s0 + sn], start=(kc == 0), stop=(kc == KC - 1))
                                return ins
                            P.op("pe", mm, r=wb[1] + [("h", kc) for kc in range(KC)], w=[("ps", pg)])
                            if typ in (0, 3):
                                P.op("act", lambda h, pg=pg, ob=ob, s0=s0, sn=sn: h.activation(
                                    out=a.ybuf[:, ob, s0:s0 + sn], in_=psA[pg][:, :sn], func=AF.Silu), r=[("ps", pg)], w=[("y", ob)])
                            else:
                                d = typ - 1
                                b = a.tcnt % 2
                                a.tcnt += 1
                                P.op("act", lambda h, pg=pg, b=b, sn=sn: h.activation(
                                    out=a.tmp[:, b, :sn], in_=psA[pg][:, :sn], func=AF.Sigmoid, scale=-1.0), r=[("ps", pg)], w=[("tmp", b)])
                                P.op("dve", lambda h, b=b, ob=ob, s0=s0, sn=sn, d=d, hh=hh: h.tensor_scalar(
                                    out=a.ybuf[:, ob, s0:s0 + sn], in0=a.tmp[:, b, :sn], scalar1=oml[:, l, d, hh:hh + 1], scalar2=None, op0=ALU.mult),
                                    r=[("tmp", b)], w=[("y", ob)])
                        if typ in (0, 3):
                            dstT = qT if typ == 0 else sgT
                            P.op("sp", lambda h, ob=ob, hh=hh, dstT=dstT: h.dma_start(out=dstT[hh * 128:(hh + 1) * 128, c0:c0 + n], in_=a.ybuf[:, ob, :n]),
                                 r=[("y", ob)], w=[("dq", typ, hh, ti)], kind="dma_out", slot=("pst", ob % 4))
                        else:
                            d = typ - 1
                            a.ycnt += 1
                            P.op("sp", lambda h, ob=ob, hh=hh, d=d: h.dma_start(out=kT[d][hh * 128:(hh + 1) * 128, c0:c0 + n], in_=a.ybuf[:, ob, :n]),
                                 r=[("y", ob)], w=[("dk", d, hh, ti)], kind="dma_out", slot=("pst", ob % 4))
                            P.op("act", lambda h, ob=ob, ob2=ob2: h.activation(out=a.ybuf[:, ob2, :n], in_=a.ybuf[:, ob, :n], func=AF.Ln, scale=-1.0, bias=1.0),
                                 r=[("y", ob)], w=[("y", ob2)])
                            P.op("sp", lambda h, ob2=ob2, hh=hh, d=d: h.dma_start(out=lfT[d][hh * 128:(hh + 1) * 128, c0:c0 + n], in_=a.ybuf[:, ob2, :n]),
                                 r=[("y", ob2)], w=[("dlf", d, hh, ti)], kind="dma_out", slot=("pst", ob2 % 4))
                    ntb = n // 128
                    for blk in range(4):
                        wb = load_w(a, w_tm[blk], 8192, ("wg", "w_in", l, 0))
                        dstD = vtok if blk < 2 else ptok
                        colo = (blk % 2) * 512
                        vb = blk % 2
                        vst = a.hid[:, 4 * vb:4 * vb + 4, :].rearrange("p a (b c) -> p (a b) c", c=512)
                        for tb in range(ntb):
                            pg = a.pscnt % 4
                            a.pscnt += 1

                            def mm(h, wb=wb, tb=tb, pg=pg):
                                ins = None
                                for kc in range(KC):
                                    ins = h.matmul(psA[pg][:, :512], lhsT=a.hbuf[:, kc, tb * 128:(tb + 1) * 128],
                                                   rhs=a.wflat[:, wb[0] + kc * 512:wb[0] + (kc + 1) * 512], start=(kc == 0), stop=(kc == KC - 1))
                                return ins
                            P.op("pe", mm, r=wb[1] + [("h", kc) for kc in range(KC)], w=[("ps", pg)])
                            eng = "act" if tb % 2 == 0 else "dve"
                            if eng == "act":
                                P.op("act", lambda h, pg=pg, tb=tb, vst=vst: h.activation(out=vst[:, tb, :], in_=psA[pg][:, :512], func=AF.Copy),
                                     r=[("ps", pg)], w=[("hid", 4 * vb + i, 0) for i in range(4)] + [("hid", 4 * vb + i, 512) for i in range(4)])
                            else:
                                P.op("dve", lambda h, pg=pg, tb=tb, vst=vst: h.tensor_copy(out=vst[:, tb, :], in_=psA[pg][:, :512]),
                                     r=[("ps", pg)], w=[("hid", 4 * vb + i, 0) for i in range(4)] + [("hid", 4 * vb + i, 512) for i in range(4)])
                        P.op("sp", lambda h, vst=vst, dstD=dstD, colo=colo, ntb=ntb: h.dma_start(
                            out=dstD[c0:c0 + n, colo:colo + 512].rearrange("(tb p) c -> p tb c", p=128), in_=vst[:, :ntb, :]),
                            r=[("hid", 4 * vb + i, 0) for i in range(4)] + [("hid", 4 * vb + i, 512) for i in range(4)],
                            w=[("dv", blk, ti)], kind="dma_out", slot=("vstst", vb))
                for ti, tile in enumerate(TILES):
                    a_tile(ti, tile)
            P.barrier()

        def phase_b(l):
            last = (l == 1)
            with contextlib.ExitStack() as stB:
                def sbx(name, shape, dt=F32):
                    return stB.enter_context(nc.sbuf_tensor(uname(name), list(shape), dt))
                Sst = sbx("Sst", [128, 16, 128])
                Asum = sbx("Asum", [128, 16])
                lfb = sbx("lfb", [128, 2, 4, 256])
                kkb = sbx("kkb", [128, 2, 4, 256])
                qsb = sbx("qsb", [128, 2, 4, 256])
                bb = sbx("bb", [128, 4, 256])
                ub = sbx("ub", [128, 4, 256])
                dd = sbx("dd", [128, 4, 256])
                E2 = sbx("E2", [128, 4, 256])
                E3 = sbx("E3", [128, 4, 256])
                qm = sbx("qm", [128, 2, 4, 256], BF16)
                km = sbx("km", [128, 2, 4, 256], BF16)
                kmTok = sbx("kmTok", [64, 2, 16, 128], BF16)
                vt64 = sbx("vt64", [64, 2, 4, 512], BF16)
                attnT2 = sbx("attnT", [64, 2, 2, 16, 64], BF16)
                Smb = sbx("Smb", [128, 4, 2, 128], BF16)
                T1 = sbx("T1", [128, 4, 2, 128])
                gsa = sbx("gsa", [128, 2, 3, 16])
                gs = sbx("gs", [128, 2, 3, 16])
                bsum = sbx("bsum", [128, 4])
                osb = sbx("osb", [128, 4, 256])
                ofwb = sbx("ofwb", [128, 4, 256])
                sgb = sbx("sgb", [128, 4, 256])
                osq = sbx("osq", [128, 4, 256], BF16)
                rno = sbx("rno", [128, 4, 256])
                catst = sbx("catst", [128, 4, 256], BF16)
                smask = cst[:, C_SMASK:C_SMASK + 1024]
                cnt = [0]
                if deferred:
                    dstg = sbx("dstg", [128, 2, 4096])
                    dbf = sbx("dbf", [128, 2, 4096], BF16)
                dstate = {"n": 0, "prev": None}

                def emit_cast(npieces=1):
                    for _ in range(npieces):
                        if not deferred and dstate["prev"] is None:
                            return
                        n_ = dstate["n"]
                        bufi = n_ % 2
                        if deferred:
                            name, wl, wf, c0, cw = deferred.pop(0)
                            src_all = INP(name + "_t")[wl, wf]
                            P.op("sp", lambda h, bufi=bufi, c0=c0, cw=cw, src_all=src_all: h.dma_start(
                                out=dstg[:, bufi, :cw], in_=src_all[:, c0:c0 + cw]), w=[("dstg", bufi)], kind="dma_in", slot=("dstg", bufi))
                            P.op("pool", lambda h, bufi=bufi, cw=cw: h.tensor_copy(out=dbf[:, bufi, :cw], in_=dstg[:, bufi, :cw]),
                                 r=[("dstg", bufi)], w=[("dbf", bufi)])
                            cur = (name, wl, wf, c0, cw, bufi)
                        else:
                            cur = None
                        if dstate["prev"] is not None:
                            name, wl, wf, c0, cw, pb = dstate["prev"]
                            P.op("sp", lambda h, pb=pb, c0=c0, cw=cw, name=name, wl=wl, wf=wf: h.dma_start(
                                out=wgat[(name, wl, wf)][:, c0:c0 + cw], in_=dbf[:, pb, :cw]),
                                r=[("dbf", pb)], w=[("wl", name, wl, wf, c0)], kind="dma_out", slot=("dbfst", pb))
                        dstate["prev"] = cur
                        dstate["n"] = n_ + 1

                def hg_block(d, hg, t0, with_out, acc_A):
                    bi = cnt[0]
                    cnt[0] += 1
                    b = bi % 2
                    attnT = attnT2[:, d]
                    rows = slice(hg * 512, hg * 512 + 512)

                    def rowsv(Tn):
                        return Tn[rows, t0:t0 + 256].rearrange("(h p) t -> p h t", p=128)
                    P.op("sp", lambda h: h.dma_start(out=lfb[:, b], in_=rowsv(lfT[d])), w=[("lfb", b)], kind="dma_in", slot=("lfb", b))
                    P.op("sp", lambda h: h.dma_start(out=kkb[:, b], in_=rowsv(kT[d])), w=[("kkb", b)], kind="dma_in", slot=("kkb", b))
                    P.op("sp", lambda h: h.dma_start(out=vt64[:, b], in_=vtok[t0:t0 + 256, hg * 512:hg * 512 + 512].rearrange("(c p) f -> p c f", p=64)),
                         w=[("vt64", b)], kind="dma_in", slot=("vt64", b))
                    if with_out:
                        P.op("sp", lambda h: h.dma_start(out=qsb[:, b], in_=rowsv(qT)), w=[("qsb", b)], kind="dma_in", slot=("qsb", b))
                    yield "P"
                    P.op("dve", lambda h: h.tensor_tensor_scan(out=bb[:].rearrange("p h t -> p (h t)"), data0=smask,
                                                               data1=lfb[:, b].rearrange("p h t -> p (h t)"), initial=0.0,
                                                               op0=ALU.mult, op1=ALU.add), r=[("lfb", b), "cst"], w=["bb"])
                    if d == 0:
                        u = bb
                        mref = 31
                    else:
                        P.op("dve", lambda h: h.tensor_tensor(out=ub[:], in0=lfb[:, b], in1=bb[:], op=ALU.subtract), r=[("lfb", b), "bb"], w=["ub"])
                        u = ub
                        mref = 32
                    ukey = "bb" if d == 0 else "ub"
                    u4 = u[:].rearrange("p h (c t) -> p h c t", t=64)
                    b4 = bb[:].rearrange("p h (c t) -> p h c t", t=64)
                    um = u4[:, :, :, mref]
                    bt = b4[:, :, :, 63]
                    P.op("dve", lambda h: h.tensor_tensor(out=dd[:].rearrange("p h (c t) -> p h c t", t=64), in0=u4,
                                                          in1=u4[:, :, :, mref:mref + 1].to_broadcast([128, 4, 4, 64]), op=ALU.subtract),
                         r=[ukey], w=["dd"])
                    P.op("dve", lambda h: h.tensor_scalar(out=dd[:], in0=dd[:], scalar1=80.0, scalar2=-80.0, op0=ALU.min, op1=ALU.max), r=["dd"], w=["dd"])
                    yield "P"
                    ga = gsa[:, b].rearrange("p k (h c) -> p k h c", c=4)
                    if d == 0:
                        P.op("dve", lambda h: h.tensor_copy(out=ga[:, 0], in_=um), r=[ukey], w=[("gsa", b, 0)])
                        P.op("dve", lambda h: h.tensor_tensor(out=ga[:, 1], in0=bt, in1=um, op=ALU.subtract), r=[ukey, "bb"], w=[("gsa", b, 1)])
                    else:
                        P.op("dve", lambda h: h.tensor_tensor(out=ga[:, 0], in0=um, in1=bt, op=ALU.add), r=[ukey, "bb"], w=[("gsa", b, 0)])
                        P.op("dve", lambda h: h.tensor_scalar(out=ga[:, 1], in0=um, scalar1=-1.0, scalar2=None, op0=ALU.mult), r=[ukey], w=[("gsa", b, 1)])
                    P.op("dve", lambda h: h.tensor_copy(out=ga[:, 2], in_=bt), r=["bb"], w=[("gsa", b, 2)])
                    if acc_A:
                        P.op("dve", lambda h: h.tensor_reduce(out=bsum[:], in_=bt, axis=mybir.AxisListType.X, op=ALU.add), r=["bb"], w=["bsum"])
                        P.op("dve", lambda h: h.tensor_tensor(out=Asum[:, d * 8 + hg * 4:d * 8 + hg * 4 + 4], in0=Asum[:, d * 8 + hg * 4:d * 8 + hg * 4 + 4],
                                                              in1=bsum[:], op=ALU.add), r=["bsum", ("Asum", d, hg)], w=[("Asum", d, hg)])
                    P.op("act", lambda h: h.activation(out=gs[:, b], in_=gsa[:, b], func=AF.Exp),
                         r=[("gsa", b, 0), ("gsa", b, 1), ("gsa", b, 2)], w=[("gs", b)])
                    P.op("act", lambda h: h.activation(out=E3[:], in_=dd[:], func=AF.Exp, scale=-1.0), r=["dd"], w=["E3"])
                    P.op("dve", lambda h: h.tensor_tensor(out=km[:, b], in0=kkb[:, b], in1=E3[:], op=ALU.mult), r=[("kkb", b), "E3"], w=[("km", b)])
                    if with_out:
                        P.op("act", lambda h: h.activation(out=E2[:], in_=dd[:], func=AF.Exp), r=["dd"], w=["E2"])
                        P.op("dve", lambda h: h.tensor_tensor(out=qm[:, b], in0=qsb[:, b], in1=E2[:], op=ALU.mult), r=[("qsb", b), "E2"], w=[("qm", b)])
                    yield "P"
                    for rd in range(2):
                        def tr(h, rd=rd):
                            ins = None
                            for i in range(8):
                                hh = rd * 2 + i // 4
                                c = i % 4
                                ins = h.transpose(psB[0:64, i * 128:(i + 1) * 128], km[:, b, hh, c * 64:(c + 1) * 64], ident[:])
                            return ins
                        P.op("pe", tr, r=[("km", b), "ident"], w=["psB"])
                        P.op("act", lambda h, rd=rd: h.activation(out=kmTok[:, b, rd * 8:(rd + 1) * 8, :].rearrange("p i c -> p (i c)"),
                                                                    in_=psB[0:64, :], func=AF.Copy), r=["psB"], w=[("kmTok", b, rd)])
                    yield "P"
                    if with_out:
                        for rd in range(2):
                            def at(h, rd=rd):
                                ins = None
                                for i in range(8):
                                    hh = rd * 2 + i // 4
                                    c = i % 4
                                    if d == 0:
                                        h.matmul(psA[rd][0:32, i * 64:(i + 1) * 64], lhsT=km[:, b, hh, c * 64:c * 64 + 32],
                                                 rhs=qm[:, b, hh, c * 64:(c + 1) * 64], start=True, stop=True)
                                        ins = h.matmul(psA[rd][32:64, i * 64 + 32:(i + 1) * 64], lhsT=km[:, b, hh, c * 64 + 32:c * 64 + 64],
                                                       rhs=qm[:, b, hh, c * 64 + 32:(c + 1) * 64], start=True, stop=True)
                                    else:
                                        h.matmul(psA[rd][32:64, i * 64:(i + 1) * 64], lhsT=km[:, b, hh, c * 64 + 32:c * 64 + 64],
                                                 rhs=qm[:, b, hh, c * 64:(c + 1) * 64], start=True, stop=True)
                                        ins = h.matmul(psA[rd][0:32, i * 64:i * 64 + 32], lhsT=km[:, b, hh, c * 64:c * 64 + 32],
                                                       rhs=qm[:, b, hh, c * 64:c * 64 + 32], start=True, stop=True)
                                return ins
                            P.op("pe", at, r=[("km", b), ("qm", b)], w=[("psat", rd)])
                            pv = psA[rd][:, :].rearrange("p (i t) -> p i t", t=64)
                            av = attnT[:, b, rd * 8:(rd + 1) * 8, :]
                            if d == 0:
                                full_rows, part_rows, pc = slice(0, 32), slice(32, 64), slice(32, 64)
                            else:
                                full_rows, part_rows, pc = slice(32, 64), slice(0, 32), slice(0, 32)
                            P.op("dve", lambda h, pv=pv, av=av, fr=full_rows: h.tensor_tensor(
                                out=av[fr, :, :], in0=pv[fr, :, :], in1=trim[fr, d, :].unsqueeze(1).to_broadcast([32, 8, 64]), op=ALU.mult),
                                r=[("psat", rd), "trim0", "trim1"], w=[("attnT", d, b, rd)])
                            P.op("dve", lambda h, pv=pv, av=av, pr_=part_rows, pc=pc: h.tensor_tensor(
                                out=av[pr_, :, pc], in0=pv[pr_, :, pc], in1=trim[pr_, d, pc].unsqueeze(1).to_broadcast([32, 8, 32]), op=ALU.mult),
                                r=[("psat", rd), "trim0", "trim1"], w=[("attnT", d, b, rd)])
                    yield "L"
                    corder = range(4) if d == 0 else range(3, -1, -1)
                    for ci, c in enumerate(corder):
                        par = ci % 2
                        def pmm4(h, c=c, par=par):
                            ins = None
                            for hh in range(4):
                                ins = h.matmul(psA[2 + par][:, hh * 128:(hh + 1) * 128], lhsT=kmTok[0:64, b, hh * 4 + c, :],
                                               rhs=vt64[0:64, b, c, hh * 128:(hh + 1) * 128], start=True, stop=True)
                            return ins
                        P.op("pe", pmm4, r=[("kmTok", b, 0), ("kmTok", b, 1), ("vt64", b)], w=[("psP", par)])
                        for hh in range(4):
                            sidx = d * 8 + hg * 4 + hh
                            gi = hh * 4 + c
                            if with_out:
                                P.op("act", lambda h, hh=hh, par=par, sidx=sidx, gi=gi: h.activation(
                                    out=Smb[:, hh, par, :], in_=Sst[:, sidx, :], func=AF.Identity, scale=gs[:, b, 0, gi:gi + 1], bias=0.0),
                                    r=[("S", sidx), ("gs", b)], w=[("Smb", hh, par)])

                                def om(h, hh=hh, c=c, par=par):
                                    o_ap = psA[4 + hh // 2][:, (hh % 2) * 256 + c * 64:(hh % 2) * 256 + (c + 1) * 64]
                                    h.matmul(o_ap, lhsT=Smb[:, hh, par, :], rhs=qm[:, b, hh, c * 64:(c + 1) * 64], start=True, stop=False)
                                    return h.matmul(o_ap, lhsT=vt64[0:64, b, c, hh * 128:(hh + 1) * 128], rhs=attnT[0:64, b, hh * 4 + c, :],
                                                    start=False, stop=True)
                                P.op("pe", om, r=[("Smb", hh, par), ("qm", b), ("vt64", b), ("attnT", d, b, hh // 2)], w=[("pso", hh, c)])
                            P.op("act", lambda h, hh=hh, par=par, gi=gi: h.activation(
                                out=T1[:, hh, par, :], in_=psA[2 + par][:, hh * 128:(hh + 1) * 128], func=AF.Identity,
                                scale=gs[:, b, 1, gi:gi + 1], bias=0.0), r=[("psP", par), ("gs", b)], w=[("T1", hh, par)])
                            P.op("dve", lambda h, hh=hh, par=par, sidx=sidx, gi=gi: h.scalar_tensor_tensor(
                                out=Sst[:, sidx, :], in0=Sst[:, sidx, :], scalar=gs[:, b, 2, gi:gi + 1], in1=T1[:, hh, par, :],
                                op0=ALU.mult, op1=ALU.add), r=[("S", sidx), ("T1", hh, par), ("gs", b)], w=[("S", sidx)])
                        yield "C"
                    if not with_out:
                        return
                    okeys = [("pso", hh, c) for hh in range(4) for c in range(4)]
                    if d == 0:
                        P.op("act", lambda h: h.activation(out=osb[:, 0:2, :].rearrange("p h t -> p (h t)"), in_=psA[4][:, :], func=AF.Copy),
                             r=okeys[:8], w=[("osb", 0)])
                        P.op("dve", lambda h: h.tensor_copy(out=osb[:, 2:4, :].rearrange("p h t -> p (h t)"), in_=psA[5][:, :]),
                             r=okeys[8:], w=[("osb", 1)])
                        P.op("sp", lambda h: h.dma_start(out=rowsv(ofw), in_=osb[:]), r=[("osb", 0), ("osb", 1)],
                             w=[("dofw", hg, t0)], kind="dma_out", slot="osbst")
                    else:
                        P.op("sp", lambda h: h.dma_start(out=ofwb[:], in_=rowsv(ofw)), r=[("dofw", hg, t0)], w=["ofwb"], kind="dma_in", slot="ofwb")
                        P.op("sp", lambda h: h.dma_start(out=sgb[:], in_=rowsv(sgT)), w=["sgb"], kind="dma_in", slot="sgb")
                        for pr in range(2):
                            P.op("dve", lambda h, pr=pr: h.tensor_tensor(out=osb[:, 2 * pr:2 * pr + 2, :].rearrange("p h t -> p (h t)"),
                                                                         in0=psA[4 + pr][:, :], in1=ofwb[:, 2 * pr:2 * pr + 2, :].rearrange("p h t -> p (h t)"),
                                                                         op=ALU.add), r=okeys[8 * pr:8 * pr + 8] + ["ofwb"], w=[("osb", pr)])
                        P.op("act", lambda h: h.activation(out=osq[:], in_=osb[:], func=AF.Square), r=[("osb", 0), ("osb", 1)], w=["osq"])
                        for pr in range(2):
                            def nm(h, pr=pr):
                                h.matmul(psA[6][:, 0:256], lhsT=ones[:], rhs=osq[:, 2 * pr, :], start=True, stop=True)
                                return h.matmul(psA[6][:, 256:512], lhsT=ones[:], rhs=osq[:, 2 * pr + 1, :], start=True, stop=True)
                            P.op("pe", nm, r=["osq", "ones"], w=["psn"])
                            P.op("act", lambda h, pr=pr: h.activation(out=rno[:, 2 * pr:2 * pr + 2, :].rearrange("p h t -> p (h t)"), in_=psA[6][:, :],
                                                                        func=AF.Sqrt, scale=1.0 / 128, bias=eps_t[:, 0:1]), r=["psn", "eps"], w=[("rno", pr)])
                        P.op("dve", lambda h: h.reciprocal(out=rno[:], in_=rno[:]), r=[("rno", 0), ("rno", 1)], w=[("rno", 0), ("rno", 1)])
                        P.op("dve", lambda h: h.tensor_tensor(out=osb[:], in0=osb[:], in1=rno[:], op=ALU.mult),
                             r=[("osb", 0), ("osb", 1), ("rno", 0), ("rno", 1)], w=[("osb", 0), ("osb", 1)])
                        P.op("dve", lambda h: h.scalar_tensor_tensor(out=catst[:], in0=osb[:], scalar=hgain[:, l:l + 1], in1=sgb[:],
                                                                     op0=ALU.mult, op1=ALU.mult), r=[("osb", 0), ("osb", 1), "sgb", "hgain"], w=["catst"])
                        P.op("sp", lambda h: h.dma_start(out=rowsv(catT), in_=catst[:]), r=["catst"], w=[("dcat", hg, t0)], kind="dma_out", slot="catst")

                P.op("pool", lambda h: h.memset(attnT2[:], 0.0), w=[("attnT", d_, b_, rd_) for d_ in range(2) for b_ in range(2) for rd_ in range(2)])

                def zero_state():
                    P.op("pool", lambda h: h.memset(Sst[:], 0.0), r=[("S", i) for i in range(16)], w=[("S", i) for i in range(16)])

                zero_state()
                specs = []
                for d in range(2):
                    for hg in range(2):
                        specs.append((d, hg, TL, not last))
                import os as _os
                for d in range(2):
                    for hg in range(2):
                        blocks = range(TL // 256) if d == 0 else range(TL // 256 - 1, -1, -1)
                        if _os.environ.get("KDEBUG_NB") is not None:
                            nb_ = int(_os.environ["KDEBUG_NB"])
                            blocks = range(nb_) if d == 0 else range(nb_ - 1, -1, -1)
                        for bk in blocks:
                            specs.append((d, hg, bk * 256, True))
                gens = [hg_block(d_, hg_, t0_, wo_, False) for (d_, hg_, t0_, wo_) in specs]
                cur = gens[0]
                while next(cur) != "L":
                    pass
                for i_ in range(len(gens)):
                    nxt = gens[i_ + 1] if i_ + 1 < len(gens) else None
                    nxt_ready = False
                    while True:
                        try:
                            next(cur)
                        except StopIteration:
                            break
                        for _ in range(2):
                            if nxt is not None and not nxt_ready:
                                if next(nxt) == "L":
                                    nxt_ready = True
                    if nxt is not None:
                        while not nxt_ready:
                            if next(nxt) == "L":
                                nxt_ready = True
                    emit_cast()
                    cur = nxt
                while deferred or dstate["prev"] is not None:
                    emit_cast()
            P.barrier()

        def phase_c(l):
            last = (l == 1)
            ND = {0: (-1, 0), 1: (-1, 0, 1), 2: (-2, -1, 0, 1, 2), 3: tuple(range(-4, 5))}
            with contextlib.ExitStack() as stC:
                wpl = stC.enter_context(nc.sbuf_tensor(uname("wpl"), [128, 4, 2, 256], BF16))
                pm = stC.enter_context(nc.sbuf_tensor(uname("pm"), [128, NMAT, 128], BF16))
                P.op("sp", lambda h: h.dma_start(out=wpl[:], in_=wgat[("w_pool", l, 0)].rearrange("r x -> (r x)").rearrange(
                    "(p g c d) -> p g c d", p=128, g=4, c=2)), w=["wpl"], kind="dma_in", slot="wpl")
                with contextlib.ExitStack() as stH:
                    pmf = stH.enter_context(nc.sbuf_tensor(uname("pmf"), [128, NMAT, 128], F32))
                    P.op("sp", lambda h: h.dma_start(out=pmf[:], in_=INP("pmat")), w=["pmf"], kind="dma_in", slot="pmf")
                    P.op("dve", lambda h: h.tensor_copy(out=pm[:], in_=pmf[:]), r=["pmf"], w=["pm"])
                    P.barrier()
                P.op("pool", lambda h: h.memset(eps_t[:], EPS), w=["eps"])
                a = alloc_ac(stC, 16384)
                pext = a.ybuf[:, 0:8, :].bitcast(BF16).rearrange("p a (b c) -> p (a b) c", c=1024)
                dTv = a.ybuf[:, 8:12, :].bitcast(BF16).rearrange("p a (b c) -> p (a b) c", c=1024)
                invc = a.ybuf[:, 12:16, :]
                PEXT_K = [("y", i) for i in range(8)]
                DT_K = [("y", i) for i in range(8, 12)]
                INV_K = [("y", i) for i in range(12, 16)]
                wo_t2 = wtiles("w_out", l, 0, 2048)
                tiles = TILES if not last else TILES[:-1]
                def c_tile(ti, tile):
                    c0, n, wh = tile
                    sbs = subs_of(n)
                    off = 8 * ti - 4
                    P.op("sp", lambda h: h.dma_start(out=invc[:, :, :n], in_=INP("invcnt")[:, :, c0:c0 + n]), w=INV_K, kind="dma_in", slot="invc")
                    if wh == 0:
                        off = 8 * ti - 4
                        lo, hi = max(0, off), min(NBLK, off + 16)
                        P.op("sp", lambda h, lo=lo, hi=hi, off=off: h.dma_start(
                            out=pext[:, lo - off:hi - off, :], in_=ptok[lo * 128:hi * 128, :].rearrange("(b p) f -> p b f", p=128)),
                            w=PEXT_K, kind="dma_in", slot="pext")
                        groups = [(hb, [8 * ti + 4 * hb + o for o in range(4)]) for hb in range(2)]
                    else:
                        P.op("sp", lambda h: h.dma_start(out=pext[:, 0:2, :], in_=ptok[TL:T, :].rearrange("(b p) f -> p b f", p=128)),
                             w=PEXT_K, kind="dma_in", slot="pext")
                        groups = [(0, [0, 1])]
                    for fcg in range(8):
                        g = fcg // 2
                        for (hb, obs) in groups:
                            pg = a.pscnt % 4
                            a.pscnt += 1

                            def pmm(h, fcg=fcg, g=g, obs=obs, pg=pg):
                                ins = None
                                for oi, ob in enumerate(obs):
                                    if wh == 0:
                                        srcs = []
                                        for dl in ND[g]:
                                            eb = ob + dl
                                            if eb < 0 or eb >= NBLK:
                                                continue
                                            sap = pext[:, eb - off, fcg * 128:(fcg + 1) * 128]
                                            srcs.append((sap, _mat_for(midx, g, ob, dl)))
                                    else:
                                        srcs = [(pext[:, ib, fcg * 128:(fcg + 1) * 128], midx[("c", g, ob, ib)]) for ib in range(2)]
                                    for si_, (sap, mi) in enumerate(srcs):
                                        ins = h.matmul(psA[pg][:, oi * 128:(oi + 1) * 128], lhsT=sap, rhs=pm[:, mi, :],
                                                       start=(si_ == 0), stop=(si_ == len(srcs) - 1))
                                return ins
                            P.op("pe", pmm, r=PEXT_K, w=[("ps", pg)])
                            ncol = 128 * len(obs)
                            P.op("dve", lambda h, fcg=fcg, g=g, hb=hb, pg=pg, ncol=ncol: h.tensor_tensor(
                                out=dTv[:, fcg, hb * 512:hb * 512 + ncol], in0=psA[pg][:, :ncol], in1=invc[:, g, hb * 512:hb * 512 + ncol], op=ALU.mult),
                                r=[("ps", pg)] + INV_K, w=DT_K)
                    for dch in range(8):
                        g, dh = dch // 2, dch % 2
                        for (s0, sn) in sbs:
                            pg = a.pscnt % 4
                            a.pscnt += 1

                            def lmm(h, g=g, dh=dh, s0=s0, sn=sn, pg=pg):
                                h.matmul(psA[pg][:, :sn], lhsT=wpl[:, g, 0, dh * 128:(dh + 1) * 128], rhs=dTv[:, 2 * g, s0:s0 + sn], start=True, stop=False)
                                return h.matmul(psA[pg][:, :sn], lhsT=wpl[:, g, 1, dh * 128:(dh + 1) * 128], rhs=dTv[:, 2 * g + 1, s0:s0 + sn],
                                                start=False, stop=True)
                            P.op("pe", lmm, r=DT_K + ["wpl"], w=[("ps", pg)])
                            P.op("dve", lambda h, dch=dch, s0=s0, sn=sn, pg=pg: h.tensor_scalar(
                                out=a.hbuf[:, 8 + dch, s0:s0 + sn], in0=psA[pg][:, :sn], scalar1=bpool[:, l, dch:dch + 1],
                                scalar2=pscale[:, l, dch:dch + 1], op0=ALU.add, op1=ALU.mult),
                                r=[("ps", pg), "bpool", "pscale"], w=[("h", 8 + dch)])
                    if "dbg_pool" in dbg:
                        P.op("sp", lambda h: h.dma_start(out=dbg_pool_t[0][:, c0:c0 + n].rearrange("(k p) t -> p k t", p=128), in_=a.hbuf[:, 8:16, :n]),
                             r=[("h", 8 + i) for i in range(8)], w=[("dbgp", ti)], kind="dma_out", slot="dbgp")
                    P.op("sp", lambda h: h.dma_start(out=a.hbuf[:, 0:8, :n], in_=catT[0:1024, c0:c0 + n].rearrange("(k p) t -> p k t", p=128)),
                         w=[("h", i) for i in range(8)], kind="dma_in", slot="catld")
                    for m in range(KC):
                        wb = load_w(a, wo_t2[m], 2048, None)
                        for (s0, sn) in sbs:
                            pg = a.pscnt % 4
                            a.pscnt += 1

                            def omm(h, wb=wb, s0=s0, sn=sn, pg=pg):
                                ins = None
                                for kc in range(KC):
                                    ins = h.matmul(psA[pg][:, :sn], lhsT=a.wflat[:, wb[0] + kc * 128:wb[0] + (kc + 1) * 128],
                                                   rhs=a.hbuf[:, kc, s0:s0 + sn], start=(kc == 0), stop=(kc == KC - 1))
                                return ins
                            P.op("pe", omm, r=wb[1] + [("h", kc) for kc in range(KC)], w=[("ps", pg)])
                            P.op("act", lambda h, pg=pg, m=m, s0=s0, sn=sn: h.activation(out=a.ybuf[:, m, s0:s0 + sn], in_=psA[pg][:, :sn], func=AF.Copy),
                                 r=[("ps", pg)], w=[("y", m)])
                    residual(a, l, 1, tile, xs, xs, ti)
                    if last:
                        ffn(a, l, 1, tile, xs, outT, True, ti)
                    else:
                        ffn(a, l, 1, tile, xs, xs, True, ti)
                for ti, tile in enumerate(tiles):
                    c_tile(ti, tile)
            P.barrier()

        if mode == "bonly":
            phase_b(0)
            return done()
        for l in range(2):
            phase_a(l)
            if stop_after == ("a", l):
                return done()
            phase_b(l)
            if stop_after == ("b", l):
                return done()
            phase_c(l)
            if stop_after == ("c", l):
                return done()
        return done()


def make_in_maps(x, c, ctx, c_ctx, w_ada, b_ada, norm_gain, ffn_in, ffn_out, w_in, hgrn_lb, hgrn_gain,
                 w_pool, b_pool, pool_scale, w_out):
    f32 = np.float32
    x, c, ctx, c_ctx = (np.asarray(v, f32) for v in (x, c, ctx, c_ctx))
    w_ada, b_ada, norm_gain = (np.asarray(v, f32) for v in (w_ada, b_ada, norm_gain))
    ffn_in, ffn_out, w_in, w_out, w_pool = (np.asarray(v, f32) for v in (ffn_in, ffn_out, w_in, w_out, w_pool))
    hgrn_lb, hgrn_gain, b_pool, pool_scale = (np.asarray(v, f32) for v in (hgrn_lb, hgrn_gain, b_pool, pool_scale))
    tiled = _tile_weights(ffn_in, ffn_out, w_in, w_out, w_pool)
    wts = {}
    for name, E in WSPEC:
        nslot = 2 if name.startswith("ffn") else 1
        wts[name + "_t"] = np.ascontiguousarray(tiled[name].reshape(2, nslot, 128, E // 128))
    gainsT = np.ascontiguousarray(norm_gain.reshape(2, 6, KC, 128).transpose(3, 0, 1, 2))
    lbT = np.ascontiguousarray(hgrn_lb.reshape(2, 2, 8, 128).transpose(3, 0, 1, 2))
    hgainT = np.ascontiguousarray(hgrn_gain.T)
    bpoolT = np.ascontiguousarray(b_pool.reshape(2, 8, 128).transpose(2, 0, 1))
    pscaleT = np.ascontiguousarray(pool_scale.reshape(2, 8, 128).transpose(2, 0, 1))
    bada = np.ascontiguousarray(b_ada.reshape(2, 144, 128).transpose(2, 0, 1))
    wada = np.ascontiguousarray(w_ada)
    smask = np.ones(1024, f32)
    smask[::64] = 0.0
    ii = np.arange(64)
    consts = np.zeros((128, NCONST), f32)
    consts[:, C_ID:C_ID + 128] = np.eye(128, dtype=f32)
    consts[:64, C_TRIF:C_TRIF + 64] = (ii[:, None] <= ii[None, :]).astype(f32)
    consts[:64, C_TRIB:C_TRIB + 64] = (ii[:, None] >= ii[None, :]).astype(f32)
    consts[:, C_SMASK:C_SMASK + 1024] = smask[None, :]
    pmats, _, inv = _pool_tables()
    pmat = np.ascontiguousarray(pmats.transpose(1, 0, 2))
    invcnt = np.ascontiguousarray(np.broadcast_to(inv[None], (128, 4, T)))
    in_maps = []
    for b in range(NCORE):
        xT = np.ascontiguousarray(np.concatenate([x[b], ctx[b]], 0).T)
        ccT = np.ascontiguousarray(np.stack([c[b], c_ctx], 0).reshape(2, KC, 128).transpose(2, 1, 0))
        m = {"xT": xT, "ccT": ccT, "wada": wada, "bada": bada, "gainsT": gainsT, "lbT": lbT, "hgainT": hgainT,
             "bpoolT": bpoolT, "pscaleT": pscaleT, "consts": consts, "pmat": pmat, "invcnt": invcnt}
        m.update(wts)
        in_maps.append(m)
    return in_maps


_NC_CACHE = {}


def kernel(**inputs):
    in_maps = make_in_maps(**inputs)
    if "nc" not in _NC_CACHE:
        _NC_CACHE["nc"] = build_program()
    nc = _NC_CACHE["nc"]
    res = run_bass_kernel_spmd(nc, in_maps, core_ids=list(range(NCORE)))
    out = np.empty((NCORE, TL, D), np.float32)
    for b in range(NCORE):
        out[b] = res.results[b]["outT"].T
    return out
```
